# Optimizing a Trainium2 kernel written in Bass

```python
import math
import jax, jax.numpy as jnp
from jax import lax
import numpy as np

D_MODEL = 1024
BATCH = 8
SEQ = 4096
DEPTH = 1

CHUNK = 64
N_META = 16
DN_HEADS = 8
DN_DK = 128
DN_DV = 256
DN_CONV = 4
DN_QK = DN_HEADS * DN_DK
DN_V = DN_HEADS * DN_DV
SB_HEADS = 8
SB_DH = 128
SB_W = SB_HEADS * SB_DH
SB_BLOCK = 128
D_FF = -(-8 * D_MODEL // (3 * 256)) * 256
PROJ_WIDTH = 2 * DN_QK + 2 * DN_V + 2 * DN_HEADS + 3 * SB_W + 2 * D_MODEL
RMS_EPS = 1e-6
L2_EPS = 1e-6

kernel_name = 'hybrid_gdn_stickbreak_block'


def _split_points():
    widths = (DN_QK, DN_QK, DN_V, DN_V, DN_HEADS, DN_HEADS, SB_W, SB_W, SB_W, D_MODEL, D_MODEL)
    return [int(s) for s in np.cumsum(widths)[:-1]]


def rmsnorm(x, gain):
    xf = x.astype(jnp.float32)
    y = xf * lax.rsqrt(jnp.mean(xf * xf, axis=-1, keepdims=True) + RMS_EPS)
    return (y * gain.astype(jnp.float32)).astype(x.dtype)


def l2norm(x):
    xf = x.astype(jnp.float32)
    return xf * lax.rsqrt(jnp.sum(xf * xf, axis=-1, keepdims=True) + L2_EPS)


def causal_depthwise_conv(x, w):
    K, C = w.shape
    return lax.conv_general_dilated(
        x, w[:, None, :].astype(x.dtype), window_strides=(1,), padding=[(K - 1, 0)],
        dimension_numbers=('NWC', 'WIO', 'NWC'), feature_group_count=C)


def gated_delta_rule(q, k, v, g, beta):
    B, T, H, DK = q.shape
    DV = v.shape[-1]
    N = T // CHUNK
    f32 = jnp.float32

    def to_chunks(a):
        a = a.astype(f32).reshape((B, N, CHUNK, H) + a.shape[3:])
        return jnp.moveaxis(a, (1, 3), (0, 2))

    q = to_chunks(q) * (DK ** -0.5)
    k = to_chunks(k)
    v = to_chunks(v)
    beta = to_chunks(beta)
    g = jnp.cumsum(to_chunks(g), axis=-1)
    idx = jnp.arange(CHUNK)
    incl = idx[:, None] >= idx[None, :]
    strict = idx[:, None] > idx[None, :]
    decay = jnp.exp(jnp.where(incl, g[..., :, None] - g[..., None, :], -jnp.inf))
    kb = k * beta[..., None]
    lower = jnp.where(strict, jnp.einsum('nbhid,nbhjd->nbhij', kb, k) * decay, 0.0)
    eye = jnp.eye(CHUNK, dtype=f32)
    rhs = jnp.concatenate([v * beta[..., None], kb * jnp.exp(g)[..., None]], axis=-1)
    sol = lax.linalg.triangular_solve(eye + lower, rhs, left_side=True, lower=True)
    u, w = sol[..., :DV], sol[..., DV:]
    attn = jnp.einsum('nbhid,nbhjd->nbhij', q, k) * decay
    q_dec = q * jnp.exp(g)[..., None]
    k_dec = k * jnp.exp(g[..., -1:] - g)[..., None]
    g_last = jnp.exp(g[..., -1])

    def step(S, xs):
        u_c, w_c, attn_c, qd_c, kd_c, gl_c = xs
        v_new = u_c - jnp.einsum('bhcd,bhde->bhce', w_c, S)
        o = jnp.einsum('bhcd,bhde->bhce', qd_c, S) + jnp.einsum('bhij,bhje->bhie', attn_c, v_new)
        S = S * gl_c[..., None, None] + jnp.einsum('bhcd,bhce->bhde', kd_c, v_new)
        return S, o

    S0 = jnp.zeros((B, H, DK, DV), f32)
    _, o = lax.scan(step, S0, (u, w, attn, q_dec, k_dec, g_last))
    return jnp.moveaxis(o, (0, 2), (1, 3)).reshape(B, T, H, DV)


def stick_breaking_attention(q, k, v):
    B, T, H, D = q.shape
    nq = -(-T // SB_BLOCK)
    Tp = nq * SB_BLOCK
    pad = ((0, 0), (0, Tp - T), (0, 0), (0, 0))
    f32 = jnp.float32
    qh = jnp.pad(q.astype(f32), pad).reshape(B, nq, SB_BLOCK, H, D).transpose(1, 0, 3, 2, 4)
    kh = jnp.pad(k.astype(f32), pad).transpose(0, 2, 1, 3)
    vh = jnp.pad(v.astype(f32), pad).transpose(0, 2, 1, 3)
    key_pos = jnp.arange(Tp)
    scale = D ** -0.5

    def block(args):
        q_blk, start = args
        z = jnp.einsum('bhqd,bhkd->bhqk', q_blk, kh) * scale
        q_pos = start + jnp.arange(SB_BLOCK)
        visible = key_pos[None, :] < q_pos[:, None]
        log_keep = jnp.where(visible, jax.nn.log_sigmoid(-z), 0.0)
        log_w = jax.nn.log_sigmoid(z) + lax.cumsum(log_keep, axis=3, reverse=True) - log_keep
        w = jnp.where(visible, jnp.exp(log_w), 0.0)
        return jnp.einsum('bhqk,bhkd->bhqd', w, vh)

    o = lax.map(block, (qh, jnp.arange(nq) * SB_BLOCK))
    return o.transpose(1, 0, 3, 2, 4).reshape(B, Tp, H, D)[:, :T]


def hybrid_mixer(hn, w_in, conv_q, conv_k, conv_v, a_log, dt_bias, dn_gain, sbq_gain, sbk_gain,
                 w_branch_dn, w_branch_sb, w_out):
    B, T, _ = hn.shape
    proj = hn @ w_in
    (dq, dk, dv, dz, da, db, sq, sk, sv, gate_dn, gate_sb) = jnp.split(proj, _split_points(), axis=-1)

    def heads(a, d):
        return a.reshape(B, T, -1, d)

    q = l2norm(heads(jax.nn.silu(causal_depthwise_conv(dq, conv_q)), DN_DK))
    k = l2norm(heads(jax.nn.silu(causal_depthwise_conv(dk, conv_k)), DN_DK))
    v = heads(jax.nn.silu(causal_depthwise_conv(dv, conv_v)), DN_DV)
    g = -jnp.exp(a_log.astype(jnp.float32)) * jax.nn.softplus(da.astype(jnp.float32) + dt_bias.astype(jnp.float32))
    beta = jax.nn.sigmoid(db.astype(jnp.float32))
    pad_l = (-N_META) % CHUNK
    pad_r = (-(T + pad_l)) % CHUNK

    def chunk_pad(a):
        return jnp.pad(a, ((0, 0), (pad_l, pad_r)) + ((0, 0),) * (a.ndim - 2))

    o_dn = gated_delta_rule(chunk_pad(q), chunk_pad(k), chunk_pad(v), chunk_pad(g), chunk_pad(beta))
    o_dn = o_dn[:, pad_l:pad_l + T]
    o_dn = rmsnorm(o_dn, dn_gain) * jax.nn.silu(heads(dz, DN_DV).astype(jnp.float32))
    o_dn = o_dn.astype(hn.dtype).reshape(B, T, DN_V)

    qs = rmsnorm(heads(sq, SB_DH), sbq_gain)
    ks = rmsnorm(heads(sk, SB_DH), sbk_gain)
    o_sb = stick_breaking_attention(qs, ks, heads(sv, SB_DH)).astype(hn.dtype).reshape(B, T, SB_W)

    merged = jax.nn.sigmoid(gate_dn) * (o_dn @ w_branch_dn) + jax.nn.sigmoid(gate_sb) * (o_sb @ w_branch_sb)
    return merged @ w_out


def swiglu(hn, w_ffn_in, w_ffn_out):
    gate, up = jnp.split(hn @ w_ffn_in, 2, axis=-1)
    return (jax.nn.silu(gate) * up) @ w_ffn_out


def setup_inputs(seed: int = 0) -> dict:
    key = jax.random.key(seed)
    ks = jax.random.split(key, 20)
    f32 = jnp.float32

    def nrm(k, shape, fan_in):
        return jax.random.normal(k, shape, f32) * (fan_in ** -0.5)

    def gain(k, n):
        return 1.0 + 0.02 * jax.random.normal(k, (DEPTH, n), f32)

    dt = jnp.exp(jax.random.uniform(ks[7], (DEPTH, DN_HEADS), f32) * (math.log(0.1) - math.log(1e-3)) + math.log(1e-3))
    return {
        'x': jax.random.normal(ks[0], (BATCH, SEQ, D_MODEL), f32),
        'meta_tokens': jax.random.normal(ks[1], (N_META, D_MODEL), f32),
        'norm_mix_gain': gain(ks[2], D_MODEL),
        'w_in': nrm(ks[3], (DEPTH, D_MODEL, PROJ_WIDTH), D_MODEL),
        'conv_q': nrm(ks[4], (DEPTH, DN_CONV, DN_QK), DN_CONV),
        'conv_k': nrm(ks[5], (DEPTH, DN_CONV, DN_QK), DN_CONV),
        'conv_v': nrm(ks[6], (DEPTH, DN_CONV, DN_V), DN_CONV),
        'dn_a_log': jnp.log(jax.random.uniform(ks[8], (DEPTH, DN_HEADS), f32, 1.0, 16.0)),
        'dn_dt_bias': dt + jnp.log(-jnp.expm1(-dt)),
        'dn_out_norm_gain': gain(ks[9], DN_DV),
        'sb_q_norm_gain': gain(ks[10], SB_DH),
        'sb_k_norm_gain': gain(ks[11], SB_DH),
        'w_branch_dn': nrm(ks[12], (DEPTH, DN_V, D_MODEL), DN_V),
        'w_branch_sb': nrm(ks[13], (DEPTH, SB_W, D_MODEL), SB_W),
        'w_out': nrm(ks[14], (DEPTH, D_MODEL, D_MODEL), D_MODEL),
        'norm_ffn_gain': gain(ks[15], D_MODEL),
        'w_ffn_in': nrm(ks[16], (DEPTH, D_MODEL, 2 * D_FF), D_MODEL),
        'w_ffn_out': nrm(ks[17], (DEPTH, D_FF, D_MODEL), D_FF),
    }


def reference(x, meta_tokens, norm_mix_gain, w_in, conv_q, conv_k, conv_v, dn_a_log, dn_dt_bias,
              dn_out_norm_gain, sb_q_norm_gain, sb_k_norm_gain, w_branch_dn, w_branch_sb, w_out,
              norm_ffn_gain, w_ffn_in, w_ffn_out):
    B = x.shape[0]
    meta = jnp.broadcast_to(meta_tokens[None].astype(x.dtype), (B, N_META, D_MODEL))
    h = jnp.concatenate([meta, x], axis=1)
    for l in range(DEPTH):
        h = h + hybrid_mixer(rmsnorm(h, norm_mix_gain[l]), w_in[l], conv_q[l], conv_k[l], conv_v[l],
                             dn_a_log[l], dn_dt_bias[l], dn_out_norm_gain[l], sb_q_norm_gain[l],
                             sb_k_norm_gain[l], w_branch_dn[l], w_branch_sb[l], w_out[l])
        h = h + swiglu(rmsnorm(h, norm_ffn_gain[l]), w_ffn_in[l], w_ffn_out[l])
    return h[:, N_META:]
```

```python
import contextlib
import numpy as np
import concourse.bass as bass
import concourse.mybir as mybir
from concourse.bass_utils import run_bass_kernel_spmd

F32 = mybir.dt.float32
BF16 = mybir.dt.bfloat16
AF = mybir.ActivationFunctionType
ALU = mybir.AluOpType

D = 1024
KT = 8
NM = 16
DFF = 2816
NFF = DFF // 128
OFF_DQ, OFF_DK, OFF_DV, OFF_DZ, OFF_DA = 0, 1024, 2048, 4096, 6144
OFF_SQ, OFF_SK, OFF_SV, OFF_GDN, OFF_GSB = 6160, 7184, 8208, 9232, 10256
PW = 11280
RMS_EPS = 1e-6
L2_EPS = 1e-6


class Buf:
    __slots__ = ("name", "w", "r", "excl")

    def __init__(self, name="", excl=False):
        self.name = name
        self.w = None
        self.r = {}
        self.excl = excl


class Prog:
    ENGS = ("pe", "act", "dve", "pool", "sp")

    def __init__(self, nc, stack, ndma=8):
        self.nc = nc
        self.streams = {e: [] for e in self.ENGS}
        self.count = {e: 0 for e in self.ENGS}
        self.seen = {e: {} for e in self.ENGS}
        self.sems = {}
        for e in ("pe", "act", "dve", "pool"):
            self.sems[e] = stack.enter_context(nc.semaphore("s_" + e))
        self.dma_sems, self.dma_val, self.dma_rr = {}, {}, {}
        for q in ("sp", "pool"):
            keys = []
            for i in range(ndma):
                k = "d_%s%d" % (q, i)
                self.sems[k] = stack.enter_context(nc.semaphore(k))
                self.dma_val[k] = 0
                keys.append(k)
            self.dma_sems[q] = keys
            self.dma_rr[q] = 0

    def _deps(self, eng, reads, writes):
        toks = []
        for b in reads:
            if b.w is not None:
                toks.append(b.w)
        for b in writes:
            if b.w is not None:
                toks.append(b.w)
            toks.extend(b.r.items())
        seen = self.seen[eng]
        mx = {}
        for k, v in toks:
            if k == eng and eng == "pe":
                continue
            if seen.get(k, 0) >= v:
                continue
            if mx.get(k, 0) < v:
                mx[k] = v
        for k, v in mx.items():
            seen[k] = v
        return list(mx.items())

    def _mark(self, tok, reads, writes):
        k, v = tok
        for b in reads:
            if b.r.get(k, 0) < v:
                b.r[k] = v
        for b in writes:
            b.w = tok
            b.r = {}

    @staticmethod
    def _split(reads, writes):
        xr = [b for b in reads if b.excl]
        if xr:
            reads = [b for b in reads if not b.excl]
            writes = list(writes) + xr
        return reads, writes

    def op(self, eng, fn, reads=(), writes=(), counted=True):
        reads, writes = self._split(reads, writes)
        waits = self._deps(eng, reads, writes)
        if counted:
            self.count[eng] += 1
            tok = (eng, self.count[eng])
        else:
            tok = (eng, self.count[eng] + 1)
        self.streams[eng].append((waits, fn, (eng, 1) if counted else None))
        self._mark(tok, reads, writes)

    def dma(self, q, out, in_, reads=(), writes=(), **kw):
        keys = self.dma_sems[q]
        k = keys[self.dma_rr[q] % len(keys)]
        self.dma_rr[q] += 1
        waits = self._deps(q, reads, writes)
        prev = self.dma_val[k]
        if prev > 0 and self.seen[q].get(k, 0) < prev:
            self.seen[q][k] = prev
            waits.append((k, prev))
        self.dma_val[k] = prev + 16
        tok = (k, prev + 16)

        def fn(e, out=out, in_=in_, kw=kw):
            return e.dma_start(out=out, in_=in_, **kw)
        self.streams[q].append((waits, fn, (k, 16)))
        self._mark(tok, reads, writes)

    def barrier(self):
        allw = [(e, self.count[e]) for e in ("pe", "act", "dve", "pool") if self.count[e] > 0]
        allw += [(k, v) for k, v in self.dma_val.items() if v > 0]
        for e in self.ENGS:
            waits = []
            for k, v in allw:
                if k == e:
                    continue
                if self.seen[e].get(k, 0) < v:
                    self.seen[e][k] = v
                    waits.append((k, v))
            self.streams[e].append((waits, None, None))

    def emit(self):
        nc = self.nc
        with nc.Block() as block:
            def run(eng):
                def body(e):
                    for waits, fn, inc in self.streams[eng]:
                        for k, v in waits:
                            e.wait_ge(self.sems[k], v)
                        if fn is None:
                            continue
                        ins = fn(e)
                        if inc is not None:
                            ins.then_inc(self.sems[inc[0]], inc[1])
                return body
            block.tensor(run("pe"))
            block.scalar(run("act"))
            block.vector(run("dve"))
            block.gpsimd(run("pool"))
            block.sync(run("sp"))


class Rot:
    def __init__(self, alloc, name, shape, dt, n, excl=False):
        self.items = [(alloc("%s%d" % (name, i), shape, dt), Buf("%s%d" % (name, i), excl)) for i in range(n)]
        self.i = 0

    def next(self):
        it = self.items[self.i % len(self.items)]
        self.i += 1
        return it


def build(NT, do_dn=True, do_sb=True, dbg=False):
    NPOS = NM + NT
    NTT = NT // 128
    NCH = 1 + NTT
    NG = NT // 512
    nc = bass.Bass("TRN2", target_bir_lowering=False)

    def dram(name, shape, dt=F32, kind="ExternalInput"):
        return nc.dram_tensor(name, shape, dt, kind=kind).ap()

    x = dram("x", [NT, D])
    meta = dram("meta", [NM, D])
    g_mix = dram("g_mix", [D])
    w_in = dram("w_in", [D, PW])
    conv_q = dram("conv_q", [4, 1024])
    conv_k = dram("conv_k", [4, 1024])
    conv_v = dram("conv_v", [4, 2048])
    a_log = dram("a_log", [8])
    dt_bias = dram("dt_bias", [8])
    g_dn = dram("g_dn", [256])
    g_sbq = dram("g_sbq", [128])
    g_sbk = dram("g_sbk", [128])
    w_bdn = dram("w_bdn", [2048, D])
    w_bsb = dram("w_bsb", [1024, D])
    w_out = dram("w_out", [D, D])
    g_ffn = dram("g_ffn", [D])
    w_fi = dram("w_fi", [D, 2 * DFF])
    w_fo = dram("w_fo", [DFF, D])
    out = dram("out", [NT, D], F32, kind="ExternalOutput")
    okind = "ExternalOutput" if dbg else "Internal"
    odnT = dram("odnT", [2048, NT], BF16, kind=okind)
    osbT = dram("osbT", [1024, NT], BF16, kind=okind)
    h1 = dram("h1", [NT, D], F32, kind=okind)

    with contextlib.ExitStack() as glob:
        P = Prog(nc, glob)

        def alloc_in(st):
            def sb(name, shape, dt):
                return st.enter_context(nc.sbuf_tensor(name, shape, dt))

            def ps(name, shape, dt):
                return st.enter_context(nc.psum_tensor(name, shape, dt))
            return sb, ps

        gsb, _ = alloc_in(glob)

        def mm(o, lhsT, rhs, start, stop, rd, wr, counted=None, skip=False):
            if counted is None:
                counted = stop
            P.op("pe", lambda e: e.matmul(o, lhsT=lhsT, rhs=rhs, start=start, stop=stop, skip_group_check=skip),
                 rd, wr, counted)

        def tr(o, in_, ident, rd, wr, counted=True):
            P.op("pe", lambda e: e.transpose(out=o, in_=in_, identity=ident), rd, wr, counted)

        def act(o, in_, func, rd, wr, scale=1.0, bias=None, accum=None):
            def fn(e):
                kw = {}
                if bias is not None:
                    kw["bias"] = bias
                if accum is not None:
                    kw["accum_out"] = accum
                return e.activation(out=o, in_=in_, func=func, scale=scale, **kw)
            P.op("act", fn, rd, wr)

        def ts(eng, o, in0, s1, op0, rd, wr, s2=None, op1=None):
            def fn(e):
                if op1 is None:
                    return e.tensor_scalar(out=o, in0=in0, scalar1=s1, scalar2=None, op0=op0)
                return e.tensor_scalar(out=o, in0=in0, scalar1=s1, scalar2=s2, op0=op0, op1=op1)
            P.op(eng, fn, rd, wr)

        def tt(eng, o, in0, in1, op, rd, wr):
            P.op(eng, lambda e: e.tensor_tensor(out=o, in0=in0, in1=in1, op=op), rd, wr)

        def stt(o, in0, scalar, in1, op0, op1, rd, wr):
            P.op("dve", lambda e: e.scalar_tensor_tensor(out=o, in0=in0, scalar=scalar, in1=in1, op0=op0, op1=op1), rd, wr)

        def cp(eng, o, in_, rd, wr):
            if eng == "act":
                act(o, in_, AF.Copy, rd, wr)
            else:
                P.op(eng, lambda e: e.tensor_copy(out=o, in_=in_), rd, wr)

        def memset(eng, o, val, wr):
            P.op(eng, lambda e: e.memset(o, val), (), wr)

        def asel(o, pattern, cmp, fill, cm, rd_wr, base=0):
            P.op("pool", lambda e: e.affine_select(out=o, in_=o, pattern=pattern, compare_op=cmp, fill=fill,
                                                   base=base, channel_multiplier=cm), rd_wr, rd_wr)

        identf = gsb("identf", [128, 128], F32)
        identb = gsb("identb", [128, 128], BF16)
        UT = gsb("UT", [128, 128], F32)
        SL = gsb("SL", [128, 128], F32)
        MASKS = gsb("MASKS", [128, 256], F32)
        ONESf = gsb("ONESf", [128, 128], F32)
        ONESb = gsb("ONESb", [128, 128], BF16)
        TRIb = gsb("TRIb", [128, 128], BF16)
        Zb = gsb("Zb", [128, 128], BF16)
        mhalf = gsb("mhalf", [128, 2], F32)
        Bc = Buf("consts")
        memset("pool", identf[:], 0.0, [Bc])
        asel(identf[:], [[-1, 128]], ALU.not_equal, 1.0, 1, [Bc])
        memset("pool", UT[:], 1.0, [Bc])
        asel(UT[:], [[1, 128]], ALU.is_ge, 0.0, -1, [Bc])
        memset("pool", SL[:], 1.0, [Bc])
        asel(SL[:], [[-1, 128]], ALU.is_gt, 0.0, 1, [Bc])
        memset("pool", ONESf[:], 1.0, [Bc])
        memset("pool", Zb[:], 0.0, [Bc])
        memset("pool", mhalf[:], -0.5, [Bc])
        cp("dve", identb[:], identf[:], [Bc], [Bc])
        cp("dve", ONESb[:], ONESf[:], [Bc], [Bc])
        cp("dve", MASKS[:, 0:128], SL[:], [Bc], [Bc])
        cp("dve", MASKS[:, 128:256], UT[:], [Bc], [Bc])
        trif = gsb("trif", [128, 128], F32)
        memset("pool", trif[:], 1.0, [Bc])
        asel(trif[:], [[-1, 128]], ALU.is_ge, 0.0, 1, [Bc])
        cp("dve", TRIb[:], trif[:], [Bc], [Bc])

        UTb = gsb("UTb", [128, 128], BF16)
        SLb = gsb("SLb", [128, 128], BF16)
        SLx = gsb("SLx", [128, 129], F32)
        UTx = gsb("UTx", [128, 129], F32)
        MASKX = gsb("MASKX", [128, 258], F32)
        cp("dve", UTb[:], UT[:], [Bc], [Bc])
        cp("dve", SLb[:], SL[:], [Bc], [Bc])
        memset("pool", SLx[:], 1.0, [Bc])
        memset("pool", UTx[:], 1.0, [Bc])
        memset("pool", MASKX[:], 1.0, [Bc])
        cp("dve", SLx[:, 0:128], SL[:], [Bc], [Bc])
        cp("dve", UTx[:, 0:128], UT[:], [Bc], [Bc])
        ts("dve", MASKX[:, 0:128], SL[:], -1.0, ALU.mult, [Bc], [Bc])
        SUX = gsb("SUX", [128, 2, 130], F32)
        memset("pool", SUX[:], 0.0, [Bc])
        cp("dve", SUX[:, 0, 0:129], SLx[:], [Bc], [Bc])
        cp("dve", SUX[:, 1, 0:129], UTx[:], [Bc], [Bc])
        cp("dve", MASKX[:, 129:257], UT[:], [Bc], [Bc])
        BLK = gsb("BLK", [128, 6, 128], F32)
        with contextlib.ExitStack() as tmpst:
            tsb, tps = alloc_in(tmpst)
            Ab = tsb("Ab", [128, 128], F32)
            Eb = tsb("Eb", [128, 5, 128], F32)
            pE = tps("pE", [128, 512], F32)
            BA, BE, BpE = Buf(), Buf(), Buf("pE", True)
            for li, b in enumerate((4, 8, 16, 32, 64)):
                memset("pool", Ab[:], 1.0, [BA])
                asel(Ab[:], [[1, 128]], ALU.is_ge, 0.0, -b, [BA])
                asel(Ab[:], [[-1, 128]], ALU.is_ge, 0.0, b, [BA], base=b - 1)
                nr = 128 // b
                mm(pE[:, 0:128], Ab[0:nr, :], Ab[0:nr, :], True, True, [BA], [BpE])
                cp("dve", Eb[:, li, :], pE[:, 0:128], [BpE], [BE])
            cp("dve", BLK[:, 0, :], Eb[:, 0, :], [BE], [Bc])
            for li in range(1, 5):
                tt("dve", BLK[:, li, :], Eb[:, li, :], Eb[:, li - 1, :], ALU.subtract, [BE], [Bc])
            tt("dve", BLK[:, 5, :], ONESf[:], Eb[:, 4, :], ALU.subtract, [BE, Bc], [Bc])
            P.barrier()
        mid = contextlib.ExitStack()
        msb, _ = alloc_in(mid)
        hnT = msb("hnT", [128, KT, NPOS], BF16)
        ab_all = msb("ab_all", [128, NCH, 16], F32)
        P.barrier()

        def rmsnorm_T_gen(sbx, psx, tag, n_rows, xrow_ap, gcol, dst_fn, R, Bx, Bg):
            junk, bj = R["junk"].next()
            ss, bs = R["ss"].next()
            act(junk[0:n_rows, :], xrow_ap, AF.Square, [Bx], [bj, bs], accum=ss[0:n_rows, 0:1])
            yield
            ts("pool", ss[0:n_rows, 0:1], ss[0:n_rows, 0:1], 1.0 / D, ALU.mult, [bs], [bs], s2=RMS_EPS, op1=ALU.add)
            yield
            tt("pool", ss[0:n_rows, 0:1], ss[0:n_rows, 0:1], mhalf[0:n_rows, 0:1], ALU.pow, [bs], [bs])
            yield
            xs, bxs = R["xs"].next()
            ts("dve", xs[0:n_rows, :], xrow_ap, ss[0:n_rows, 0:1], ALU.mult, [Bx, bs], [bxs])
            yield
            pT, bp = R["pT"].next()
            for k in range(KT):
                tr(pT[:, k * 128:k * 128 + n_rows], xs[0:n_rows, k * 128:(k + 1) * 128], identb[0:n_rows, 0:n_rows],
                   [bxs], [bp], counted=(k == KT - 1))
            yield
            for k in range(KT):
                o_ap, wr = dst_fn(k)
                ts("dve", o_ap, pT[:, k * 128:k * 128 + n_rows], gcol[:, k:k + 1], ALU.mult, [bp, Bg], wr)

        def rmsnorm_T(*a_, **k_):
            for _ in rmsnorm_T_gen(*a_, **k_):
                pass

        with contextlib.ExitStack() as ph:
            sb, ps = alloc_in(ph)
            g1 = sb("g1", [128, KT], F32)
            Bg1 = Buf()
            P.dma("sp", g1[:], g_mix.rearrange("(k p) -> p k", p=128), writes=[Bg1], allow_slow_non_contiguous=True)
            wab = sb("wab", [128, KT, 16], BF16)
            Bwab = Buf()
            P.dma("pool", wab[:], w_in[:, OFF_DA:OFF_DA + 16].rearrange("(k p) n -> p k n", p=128), writes=[Bwab])
            alb = sb("alb", [128, 8], F32)
            dtb = sb("dtb", [128, 8], F32)
            Bal = Buf()
            P.dma("sp", alb[:], a_log.partition_broadcast(128), writes=[Bal])
            P.dma("sp", dtb[:], dt_bias.partition_broadcast(128), writes=[Bal])
            act(alb[:], alb[:], AF.Exp, [Bal], [Bal])
            ts("dve", alb[:], alb[:], -1.0, ALU.mult, [Bal], [Bal])
            R = {"junk": Rot(sb, "junkA", [128, D], BF16, 1), "ss": Rot(sb, "ssA", [128, 1], F32, 4),
                 "xs": Rot(sb, "xsA", [128, D], BF16, 4), "pT": Rot(ps, "pTA", [128, D], BF16, 3, excl=True)}
            xR = Rot(sb, "xA", [128, D], F32, 4)
            pab = Rot(ps, "pab", [128, 512], F32, 3, excl=True)
            tmpR = Rot(sb, "tmpA", [128, 16], F32, 4)
            def tileA(c):
                n = NM if c == 0 else 128
                p0 = 0 if c == 0 else NM + 128 * (c - 1)
                xt, bx = xR.next()
                src = meta if c == 0 else x[128 * (c - 1):128 * c, :]
                P.dma("sp", xt[0:n, :], src, writes=[bx])
                bh = Buf()

                def dst(k, p0=p0, n=n, bh=bh):
                    return hnT[:, k, p0:p0 + n], [bh]
                yield from rmsnorm_T_gen(sb, ps, "A", n, xt[0:n, :], g1, dst, R, bx, Bg1)
                yield
                pa, bpa = pab.next()
                for k in range(KT):
                    mm(pa[0:n, 0:16], hnT[:, k, p0:p0 + n], wab[:, k, :], k == 0, k == KT - 1, [bh, Bwab], [bpa])
                yield
                t, bt = tmpR.next()
                tt("dve", t[0:n, 0:8], pa[0:n, 0:8], dtb[0:n, :], ALU.add, [bpa, Bal], [bt])
                yield
                act(t[0:n, 0:8], t[0:n, 0:8], AF.Exp, [bt], [bt])
                act(t[0:n, 8:16], pa[0:n, 8:16], AF.Exp, [bpa, bt], [bt], scale=-1.0)
                yield
                act(t[0:n, 0:8], t[0:n, 0:8], AF.Ln, [bt], [bt], bias=1.0)
                ts("dve", t[0:n, 8:16], t[0:n, 8:16], 1.0, ALU.add, [bt], [bt])
                yield
                tt("dve", ab_all[0:n, c, 0:8], t[0:n, 0:8], alb[0:n, :], ALU.mult, [bt, Bal], [bt])
                P.op("dve", lambda e, n=n, t=t, c=c: e.reciprocal(out=ab_all[0:n, c, 8:16], in_=t[0:n, 8:16]), [bt], [bt])

            NLA = 3
            todoA = list(range(NCH))
            actA = []
            while todoA or actA:
                while len(actA) < NLA and todoA:
                    actA.append(tileA(todoA.pop(0)))
                for g_ in list(actA):
                    try:
                        next(g_)
                    except StopIteration:
                        actA.remove(g_)
            P.barrier()

        if do_dn:
            with contextlib.ExitStack() as ph:
                sb, ps = alloc_in(ph)
                G = 6
                NRC = G + 2
                WhR = Rot(sb, "Wh", [128, KT, 768], BF16, 2)
                cw = sb("cw", [128, 16], F32)
                dg = sb("dg", [128, 16, 128], BF16)
                Bcw, Bdg = Buf(), Buf()
                gz = sb("gz", [128, 256], F32)
                Bgz = Buf()
                P.dma("sp", gz[:], g_dn.partition_broadcast(128), writes=[Bgz])
                ts("dve", gz[:], gz[:], 0.5, ALU.mult, [Bgz], [Bgz])
                S_f = Rot(sb, "Sf", [128, 256], F32, 2)
                S_b = Rot(sb, "Sb", [128, 256], BF16, 2)
                ostg = Rot(sb, "ostg", [128, 2, 512], BF16, 2)
                slotP = [ps("slotP%d" % i, [128, 512], F32) for i in range(G)]
                slotB = [Buf("slotP%d" % i, True) for i in range(G)]
                pS1R = Rot(ps, "pS1", [128, 512], F32, 1, excl=True)
                pS2, BS2 = ps("pS2", [128, 512], F32), Buf("pS2", True)
                pS2b = pS2[:].bitcast(BF16)
                SL_ = []
                for i in range(G):
                    d_ = {}
                    for (nm, shp, dt_) in (("s2", [128, 512], F32), ("sc", [128, 24], F32), ("r4", [128, 4, 130], BF16), ("ahl", [128, 2], BF16),
                                           ("EX", [128, 258], F32), ("tokp", [128, 4, 128], BF16), ("bv", [128, 256], BF16),
                                           ("TQ", [128, 2, 128], BF16), ("M", [128, 128], F32), ("Mmb", [128, 6, 128], BF16),
                                           ("N0b", [128, 128], BF16), ("M1b", [128, 128], BF16),
                                           ("Q", [128, 128], BF16), ("X", [128, 128], BF16),
                                           ("Yb0", [128, 128], BF16), ("Yb1", [128, 128], BF16),
                                           ("rawc", [128, 4, 132], BF16), ("junk2", [128, 256], BF16)):
                        d_[nm] = (sb("%s_%d" % (nm, i), shp, dt_), Buf("%s_%d" % (nm, i)))
                    d_["r4b"] = [Buf(), Buf()]
                    d_["junk2b"] = [Buf(), Buf()]
                    d_["sc2b"] = Buf()
                    d_["tokb"] = [Buf() for _ in range(4)]
                    d_["TQb"] = [Buf(), Buf()]
                    SL_.append(d_)
                junk, bjk = sb("junkB", [128, 256], BF16), Buf()
                c_kd = Rot(sb, "c_kd", [128, 128], BF16, NRC)
                c_qdT = Rot(sb, "c_qdT", [128, 128], BF16, NRC)
                c_at = Rot(sb, "c_at", [128, 128], BF16, NRC)
                c_u = Rot(sb, "c_u", [128, 256], F32, NRC)
                c_wT = Rot(sb, "c_wT", [128, 128], BF16, NRC)
                c_zsg = Rot(sb, "c_zsg", [128, 256], BF16, NRC)
                c_ex3 = Rot(sb, "c_ex3", [128, 4], F32, NRC)
                r_vn = Rot(sb, "vn", [128, 256], BF16, 2)
                r_so = Rot(sb, "so", [128, 1], F32, 2)
                r_oo = Rot(sb, "oo", [128, 256], BF16, 2)

                def load_head_w(h):
                    W, bW = WhR.next()
                    for (o0, src0, wdt) in ((0, OFF_DQ + h * 128, 128), (128, OFF_DK + h * 128, 128),
                                            (256, OFF_DV + h * 256, 256), (512, OFF_DZ + h * 256, 256)):
                        P.dma("pool", W[:, :, o0:o0 + wdt], w_in[:, src0:src0 + wdt].rearrange("(k p) n -> p k n", p=128),
                              writes=[bW])
                    return W, bW

                ready = {}

                def prep(h, c, si, W, bW, key, dgt):
                    dg, Bdg = dgt
                    C = NM if c == 0 else 128
                    p0 = 0 if c == 0 else NM + 128 * (c - 1)
                    Pp, BP = slotP[si], slotB[si]
                    Ppb = Pp[:].bitcast(BF16)
                    d_ = SL_[si]
                    s2, bs2 = d_["s2"]
                    sc, bsc = d_["sc"]
                    r4, _ = d_["r4"]
                    br4s = d_["r4b"]
                    junk2, _ = d_["junk2"]
                    bj2a, bj2b = d_["junk2b"]
                    bsc2 = d_["sc2b"]
                    btoks = d_["tokb"]
                    bTQq, bTQk = d_["TQb"]
                    ahl, bahl = d_["ahl"]
                    Mmb, bMmb = d_["Mmb"]
                    EX, bEX = d_["EX"]
                    tokp, btok = d_["tokp"]
                    bv, bbv = d_["bv"]
                    TQ, bTQ = d_["TQ"]
                    M, bM = d_["M"]
                    N0b, bN0b = d_["N0b"]
                    M1b, bM1b = d_["M1b"]
                    Q, bQ = d_["Q"]
                    X, bX = d_["X"]
                    Ybs = [d_["Yb0"], d_["Yb1"]]
                    rawc, Braw = d_["rawc"]
                    zt, bzt = d_["s2"][0][:, 0:256], d_["s2"][1]
                    kd, bkd = c_kd.next()
                    qdT, bqdT = c_qdT.next()
                    at, bat = c_at.next()
                    u, bu = c_u.next()
                    wT, bwT = c_wT.next()
                    zsg, bzsg = c_zsg.next()
                    ex3, bex3 = c_ex3.next()
                    if c == 0:
                        memset("pool", rawc[:, :, 0:3], 0.0, [Braw])
                        hs0, nr_, ro = 0, C, 3
                    else:
                        hs0, nr_, ro = p0 - 3, C + 3, 0
                    for (cts, eng_) in (((0, 1, 2), "act"), ((3,), "dve")):
                        for i3, ct in enumerate(cts):
                            for k in range(KT):
                                mm(Pp[:, i3 * nr_:(i3 + 1) * nr_], W[:, k, ct * 128:(ct + 1) * 128], hnT[:, k, hs0:hs0 + nr_],
                                   k == 0, k == KT - 1, [bW], [BP])
                        zdo = (c > 0 and cts == (3,))
                        if zdo:
                            for k in range(KT):
                                mm(Pp[0:C, 192:448], hnT[:, k, p0:p0 + C], W[:, k, 512:768], k == 0, k == KT - 1, [bW], [BP])
                        yield
                        for i3, ct in enumerate(cts):
                            cp(eng_, rawc[:, ct, ro:ro + nr_], Pp[:, i3 * nr_:(i3 + 1) * nr_], [BP], [Braw])
                        if zdo:
                            act(zt[0:C, :], Pp[0:C, 192:448], AF.Tanh, [BP], [bzt], scale=0.5)
                        yield
                        if zdo:
                            stt(zt[0:C, :], zt[0:C, :], 1.0, Pp[0:C, 192:448], ALU.add, ALU.mult, [bzt, BP], [bzt])
                            yield
                            tt("dve", zsg[0:C, :], zt[0:C, :], gz[0:C, :], ALU.mult, [bzt, Bgz], [bzsg])
                    for (ct, o0) in ((0, 0), (1, 128), (2, 256), (3, 384)):
                        if c == 0 and ct == 0:
                            continue
                        for j in range(4):
                            mm(Pp[0:C, o0:o0 + 128], rawc[:, ct, j:j + C], dg[:, ct * 4 + j, :], j == 0, j == 3,
                               [Braw, Bdg], [BP])
                    yield
                    c0 = 128 if c == 0 else 0
                    act(s2[0:C, c0:512], Pp[0:C, c0:512], AF.Tanh, [BP], [bs2], scale=0.5)
                    yield
                    stt(s2[0:C, c0:512], s2[0:C, c0:512], 1.0, Pp[0:C, c0:512], ALU.add, ALU.mult, [bs2, BP], [bs2])
                    yield
                    if c > 0:
                        act(junk2[0:C, 0:128], s2[0:C, 0:128], AF.Square, [bs2], [bj2a, bsc], accum=sc[0:C, 0:1])
                    else:
                        memset("dve", sc[0:C, 0:1], 1.0, [bsc])
                    act(junk2[0:C, 128:256], s2[0:C, 128:256], AF.Square, [bs2], [bj2b, bsc], accum=sc[0:C, 1:2])
                    cp("dve", ahl[0:C, 0:1], ab_all[0:C, c, h:h + 1], [], [bahl])
                    yield
                    ts("pool", sc[0:C, 0:2], sc[0:C, 0:2], 4.0 * L2_EPS, ALU.add, [bsc], [bsc])
                    tt("dve", ahl[0:C, 1:2], ab_all[0:C, c, h:h + 1], ahl[0:C, 0:1], ALU.subtract, [bahl], [bahl])
                    yield
                    tt("pool", sc[0:C, 0:2], sc[0:C, 0:2], mhalf[0:C, 0:2], ALU.pow, [bsc], [bsc])
                    a_col = ab_all[0:C, c, h:h + 1]
                    b_col = ab_all[0:C, c, 8 + h:9 + h]
                    for i2 in range(2):
                        ts("dve", r4[0:C, 2 * i2:2 * i2 + 2, :], SUX[0:C, :, :], ahl[0:C, i2:i2 + 1], ALU.mult, [Bc, bahl],
                           [br4s[i2]])
                    yield
                    mm(Pp[0:C, 0:129], UTb[0:C, 0:C], r4[0:C, 0, 0:129], True, False, [br4s[0], Bc], [BP], counted=False)
                    mm(Pp[0:C, 0:129], UTb[0:C, 0:C], r4[0:C, 2, 0:129], False, True, [br4s[1], Bc], [BP])
                    mm(Pp[0:C, 129:258], SLb[0:C, 0:C], r4[0:C, 1, 0:129], True, False, [br4s[0], Bc], [BP], counted=False)
                    mm(Pp[0:C, 129:258], SLb[0:C, 0:C], r4[0:C, 3, 0:129], False, True, [br4s[1], Bc], [BP])
                    mm(Pp[:, 258:259], ONESb[0:C, :], ahl[0:C, 0:1], True, False, [bahl, Bc], [BP], counted=False)
                    mm(Pp[:, 258:259], ONESb[0:C, :], ahl[0:C, 1:2], False, True, [bahl, Bc], [BP])
                    yield
                    act(EX[0:C, :], Pp[0:C, 0:258], AF.Exp, [BP], [bEX])
                    act(ex3[:, 0:1], Pp[:, 258:259], AF.Exp, [BP], [bex3])
                    yield
                    tt("dve", EX[0:C, :], EX[0:C, :], MASKX[0:C, :], ALU.mult, [bEX, Bc], [bEX])
                    yield
                    tt("dve", sc[0:C, 2:3], b_col, EX[0:C, 128:129], ALU.mult, [bEX], [bsc2])
                    if c > 0:
                        ts("dve", tokp[0:C, 0, :], s2[0:C, 0:128], sc[0:C, 0:1], ALU.mult, [bs2, bsc], [btoks[0]])
                        ts("dve", tokp[0:C, 1, :], s2[0:C, 0:128], sc[0:C, 0:1], ALU.mult, [bs2, bsc, bEX], [btoks[1]],
                           s2=EX[0:C, 128:129], op1=ALU.mult)
                    ts("dve", tokp[0:C, 2, :], s2[0:C, 128:256], sc[0:C, 1:2], ALU.mult, [bs2, bsc], [btoks[2]])
                    ts("pool", tokp[0:C, 3, :], s2[0:C, 128:256], sc[0:C, 1:2], ALU.mult, [bs2, bsc, bsc2], [btoks[3]],
                       s2=sc[0:C, 2:3], op1=ALU.mult)
                    ts("pool", kd[0:C, :], s2[0:C, 128:256], sc[0:C, 1:2], ALU.mult, [bs2, bsc, bEX], [bkd],
                       s2=EX[0:C, 257:258], op1=ALU.mult)
                    ts("pool", bv[0:C, :], s2[0:C, 256:512], b_col, ALU.mult, [bs2], [bbv], s2=0.5, op1=ALU.mult)
                    yield
                    first = 2 if c == 0 else 0
                    for i in range(first, 3):
                        tr(Ppb[:, i * 128:i * 128 + C], tokp[0:C, i, :], identb[0:C, 0:C], [btoks[i], Bc], [BP], counted=(i == 2))
                    yield
                    if c > 0:
                        cp("act", TQ[:, 0, 0:C], Ppb[:, 0:C], [BP], [bTQq])
                        cp("dve", qdT[:, 0:C], Ppb[:, 128:128 + C], [BP], [bqdT])
                    cp("act", TQ[:, 1, 0:C], Ppb[:, 256:256 + C], [BP], [bTQk])
                    qT, kT = TQ[:, 0, 0:C], TQ[:, 1, 0:C]
                    yield
                    mm(Pp[0:C, 256:256 + C], kT, kT, True, True, [bTQk], [BP])
                    if c > 0:
                        mm(Pp[0:C, 384:384 + C], kT, qT, True, True, [bTQk, bTQq], [BP])
                    yield
                    stt(M[0:C, 0:C], Pp[0:C, 256:256 + C], b_col, EX[0:C, 0:C], ALU.mult, ALU.mult, [BP, bEX], [bM])
                    if c > 0:
                        tt("dve", at[0:C, 0:C], Pp[0:C, 384:384 + C], EX[0:C, 129:129 + C], ALU.mult, [BP, bEX], [bat])
                    yield
                    nlev = 2 if c == 0 else 5
                    tt("dve", Mmb[0:C, 0:nlev + 1, 0:C], M[0:C, 0:C].unsqueeze(1).broadcast_to([C, nlev + 1, C]),
                       BLK[0:C, 0:nlev + 1, 0:C], ALU.mult, [bM, Bc], [bMmb])
                    yield
                    tr(Ppb[0:C, 0:C], Mmb[0:C, 0, 0:C], identb[0:C, 0:C], [bMmb, Bc], [BP])
                    yield
                    cp("act", N0b[0:C, 0:C], Ppb[0:C, 0:C], [BP], [bN0b])
                    bi = 0
                    Yb, bYb = Ybs[bi]
                    tt("dve", Yb[0:C, 0:C], Ppb[0:C, 0:C], identb[0:C, 0:C], ALU.add, [BP, Bc], [bYb])
                    yield
                    mm(Pp[0:C, 128:128 + C], N0b[0:C, 0:C], Mmb[0:C, 0, 0:C], True, True, [bN0b, bMmb], [BP])
                    yield
                    cp("act", M1b[0:C, 0:C], Pp[0:C, 128:128 + C], [BP], [bM1b])
                    yield
                    mm(Pp[0:C, 256:256 + C], M1b[0:C, 0:C], Yb[0:C, 0:C], True, True, [bM1b, bYb], [BP])
                    yield
                    bi ^= 1
                    Yb2, bYb2 = Ybs[bi]
                    tt("dve", Yb2[0:C, 0:C], Pp[0:C, 256:256 + C], Yb[0:C, 0:C], ALU.add, [BP, bYb], [bYb2])
                    Yb, bYb = Yb2, bYb2
                    yield
                    for l in range(nlev):
                        mm(Pp[0:C, 256:256 + C], Mmb[0:C, 1 + l, 0:C], Yb[0:C, 0:C], True, True, [bMmb, bYb], [BP])
                        tr(Ppb[0:C, 768:768 + C], Yb[0:C, 0:C], identb[0:C, 0:C], [bYb, Bc], [BP])
                        yield
                        cp("act", Q[0:C, 0:C], Pp[0:C, 256:256 + C], [BP], [bQ])
                        cp("dve", X[0:C, 0:C], Ppb[0:C, 768:768 + C], [BP], [bX])
                        yield
                        mm(Pp[0:C, 0:C], identb[0:C, 0:C], Yb[0:C, 0:C], True, False, [bYb, Bc], [BP], counted=False)
                        mm(Pp[0:C, 0:C], X[0:C, 0:C], Q[0:C, 0:C], False, True, [bX, bQ], [BP])
                        yield
                        bi ^= 1
                        Yb2, bYb2 = Ybs[bi]
                        cp("act" if l % 2 == 0 else "dve", Yb2[0:C, 0:C], Pp[0:C, 0:C], [BP], [bYb2])
                        Yb, bYb = Yb2, bYb2
                        yield
                    mm(Pp[0:C, 0:256], Yb[0:C, 0:C], bv[0:C, :], True, True, [bYb, bbv], [BP])
                    mm(Pp[:, 256:256 + C], tokp[0:C, 3, :], Yb[0:C, 0:C], True, True, [bYb, btoks[3]], [BP])
                    yield
                    cp("act", u[0:C, :], Pp[0:C, 0:256], [BP], [bu])
                    cp("dve", wT[:, 0:C], Pp[:, 256:256 + C], [BP], [bwT])
                    yield
                    ready[key] = dict(kd=(kd, bkd), qdT=(qdT, bqdT), at=(at, bat), u=(u, bu), wT=(wT, bwT),
                                    zsg=(zsg, bzsg), ex3=(ex3, bex3))

                st_ = {}

                def scan(h, c, r_):
                    C = NM if c == 0 else 128
                    kd, bkd = r_["kd"]
                    qdT, bqdT = r_["qdT"]
                    at, bat = r_["at"]
                    u, bu = r_["u"]
                    wT, bwT = r_["wT"]
                    zsg, bzsg = r_["zsg"]
                    ex3, bex3 = r_["ex3"]
                    if c == 0:
                        Sf, bSf = S_f.next()
                        Sb, bSb = S_b.next()
                        memset("dve", Sf[:], 0.0, [bSf])
                        memset("dve", Sb[:], 0.0, [bSb])
                        st_["S"] = (Sf, bSf, Sb, bSb)
                    Sf, bSf, Sb, bSb = st_["S"]
                    pS1, BS1 = pS1R.next()
                    mm(pS1[0:C, 0:256], wT[:, 0:C], Sb[:], True, True, [bwT, bSb], [BS1])
                    yield
                    vn, bvn = r_vn.next()
                    tt("dve", vn[0:C, :], u[0:C, :], pS1[0:C, 0:256], ALU.subtract, [bu, BS1], [bvn])
                    yield
                    if c > 0:
                        mm(pS1[0:C, 256:512], qdT[:, 0:C], Sb[:], True, False, [bqdT, bSb], [BS1], counted=False)
                        mm(pS1[0:C, 256:512], at[0:C, 0:C], vn[0:C, :], False, True, [bat, bvn], [BS1])
                    mm(pS2[:, 0:256], kd[0:C, :], vn[0:C, :], True, True, [bkd, bvn], [BS2])
                    yield
                    Sf2, bSf2 = S_f.next()
                    stt(Sf2[:], Sf[:], ex3[:, 0:1], pS2[:, 0:256], ALU.mult, ALU.add, [bSf, bex3, BS2], [bSf2])
                    yield
                    Sb2, bSb2 = S_b.next()
                    cp("act", Sb2[:], Sf2[:], [bSf2], [bSb2])
                    st_["S"] = (Sf2, bSf2, Sb2, bSb2)
                    st_["state_done"] = True
                    if c > 0:
                        so, bso = r_so.next()
                        act(junk[0:C, :], pS1[0:C, 256:512], AF.Square, [BS1], [bjk, bso], accum=so[0:C, 0:1])
                        yield
                        ts("pool", so[0:C, 0:1], so[0:C, 0:1], 1.0 / 256, ALU.mult, [bso], [bso], s2=128.0 * RMS_EPS, op1=ALU.add)
                        tt("pool", so[0:C, 0:1], so[0:C, 0:1], mhalf[0:C, 0:1], ALU.pow, [bso], [bso])
                        yield
                        oo, boo = r_oo.next()
                        stt(oo[0:C, :], pS1[0:C, 256:512], so[0:C, 0:1], zsg[0:C, :], ALU.mult, ALU.mult,
                            [BS1, bso, bzsg], [boo])
                        yield
                        q4 = (c - 1) % 4
                        if q4 == 0:
                            st_["stg"] = ostg.next()
                        stg, bstg = st_["stg"]
                        for v in range(2):
                            tr(pS2b[:, 512 + v * 128:512 + v * 128 + C], oo[0:C, v * 128:(v + 1) * 128], identb[0:C, 0:C],
                               [boo, Bc], [BS2], counted=(v == 1))
                        yield
                        for v in range(2):
                            cp("act", stg[:, v, q4 * 128:(q4 + 1) * 128], pS2b[:, 512 + v * 128:512 + v * 128 + C], [BS2], [bstg])
                        if q4 == 3:
                            t0 = 128 * (c - 4)
                            P.dma("sp", odnT[h * 256:(h + 1) * 256, t0:t0 + 512].rearrange("(v p) t -> p v t", p=128),
                                  stg[:], reads=[bstg])

                dgs = [(dg, Bdg), (sb("dg_b", [128, 16, 128], BF16), Buf())]
                cws = [(cw, Bcw), (sb("cw_b", [128, 16], F32), Buf())]

                def head_setup(h):
                    cw_, Bcw_ = cws[h % 2]
                    dg_, Bdg_ = dgs[h % 2]
                    P.dma("sp", cw_[:, 0:4], conv_q[:, h * 128:(h + 1) * 128].rearrange("j c -> c j"), writes=[Bcw_],
                          allow_slow_non_contiguous=True)
                    P.dma("sp", cw_[:, 4:8], conv_k[:, h * 128:(h + 1) * 128].rearrange("j c -> c j"), writes=[Bcw_],
                          allow_slow_non_contiguous=True)
                    P.dma("sp", cw_[:, 8:12], conv_v[:, h * 256:h * 256 + 128].rearrange("j c -> c j"), writes=[Bcw_],
                          allow_slow_non_contiguous=True)
                    P.dma("sp", cw_[:, 12:16], conv_v[:, h * 256 + 128:h * 256 + 256].rearrange("j c -> c j"), writes=[Bcw_],
                          allow_slow_non_contiguous=True)
                    for i in range(16):
                        ts("dve", dg_[:, i, :], identf[:], cw_[:, i:i + 1], ALU.mult, [Bcw_, Bc], [Bdg_])

                Ws = {0: load_head_w(0), 1: load_head_w(1)}
                seq = [(h, c) for h in range(8) for c in range(NCH)]
                NSEQ = len(seq)
                ready.clear()
                active = []
                free_slots = list(range(G))
                next_i, next_scan, next_started = 0, 0, 0
                scans = []
                live = {}
                started_heads = set()
                st_["state_done"] = True
                while next_started < NSEQ or not st_["state_done"]:
                    while free_slots and next_i < NSEQ and next_i < next_scan + NRC:
                        h, c = seq[next_i]
                        if h not in Ws:
                            break
                        if h not in started_heads:
                            started_heads.add(h)
                            head_setup(h)
                        si = free_slots.pop(0)
                        W, bW = Ws[h]
                        active.append([next_i, prep(h, c, si, W, bW, next_i, dgs[h % 2]), si, h])
                        live[h] = live.get(h, 0) + 1
                        next_i += 1
                    for item in list(active):
                        try:
                            next(item[1])
                        except StopIteration:
                            active.remove(item)
                            free_slots.append(item[2])
                            hh = item[3]
                            live[hh] -= 1
                            if (live[hh] == 0 and (next_i >= NSEQ or seq[next_i][0] > hh) and hh + 2 < 8
                                    and (hh + 2) not in Ws):
                                Ws[hh + 2] = load_head_w(hh + 2)
                    for sg in list(scans):
                        try:
                            next(sg)
                        except StopIteration:
                            scans.remove(sg)
                            next_scan += 1
                    if next_started in ready and st_["state_done"]:
                        st_["state_done"] = False
                        h, c = seq[next_started]
                        scans.append(scan(h, c, ready.pop(next_started)))
                        next_started += 1
                for sg in scans:
                    for _ in sg:
                        pass
                P.barrier()

        if do_sb:
            with contextlib.ExitStack() as ph:
                sb, ps = alloc_in(ph)
                WsR = Rot(sb, "Ws", [128, KT, 384], BF16, 2)
                gq = sb("gq", [128, 1], F32)
                gk = sb("gk", [128, 1], F32)
                Bgq = Buf()
                P.dma("sp", gq[:], g_sbq.rearrange("(d o) -> d o", o=1), writes=[Bgq])
                P.dma("sp", gk[:], g_sbk.rearrange("(d o) -> d o", o=1), writes=[Bgq])
                ts("dve", gq[:], gq[:], 128.0 ** -0.5, ALU.mult, [Bgq], [Bgq])
                HB = []
                for i in range(2):
                    HB.append(dict(qT=sb("sqT%d" % i, [128, NPOS], BF16), kT=sb("skT%d" % i, [128, NPOS], BF16),
                                   nkT=sb("snkT%d" % i, [128, NPOS], BF16), vS=sb("svS%d" % i, [128, NCH, 128], BF16),
                                   Bq=Buf(), Bk=Buf(), Bv=Buf()))
                pq, Bpq = ps("pq", [128, 512], F32), Buf("pq", True)
                pss, Bpss = ps("pss", [128, 512], F32), Buf("pss", True)
                NL = 3
                lanes = []
                for li in range(NL):
                    lanes.append(dict(
                        ZC=(ps("pZC%d" % li, [128, 512], F32), Buf("pZC%d" % li, True)),
                        OT=(ps("pOT%d" % li, [128, 512], F32), Buf("pOT%d" % li, True)),
                        R=(sb("Rr%d" % li, [128, 512], BF16), Buf()),
                        e=Rot(sb, "ce%d" % li, [128, 512], F32, 2),
                        sp=Rot(sb, "csp%d" % li, [128, 512], BF16, 2),
                        w=Rot(sb, "cw%d_" % li, [128, 512], BF16, 2),
                        ot=Rot(sb, "cot%d" % li, [128, 512], BF16, 1)))
                r_raw = Rot(sb, "craw", [128, 512], F32, 2)
                r_sq = Rot(sb, "csq", [128, 512], BF16, 2)
                r_ln = Rot(sb, "cln", [128, 512], F32, 2)

                def load_sb_w(h):
                    W, bW = WsR.next()
                    for (o0, src0) in ((0, OFF_SQ + h * 128), (128, OFF_SK + h * 128), (256, OFF_SV + h * 128)):
                        P.dma("pool", W[:, :, o0:o0 + 128], w_in[:, src0:src0 + 128].rearrange("(k p) n -> p k n", p=128),
                              writes=[bW])
                    return W, bW

                def proj(h, W, bW):
                    H = HB[h % 2]
                    kT, nkT, vS = H["kT"], H["nkT"], H["vS"]
                    Bv = H["Bv"]
                    for (which, o0, dstT, gcol, bd) in (("q", 0, H["qT"], gq, H["Bq"]), ("k", 128, H["kT"], gk, H["Bk"])):
                        for p0 in range(0, NPOS, 512):
                            n = min(512, NPOS - p0)
                            for k in range(KT):
                                mm(pq[:, 0:n], W[:, k, o0:o0 + 128], hnT[:, k, p0:p0 + n], k == 0, k == KT - 1, [bW], [Bpq])
                            yield
                            rw, brw = r_raw.next()
                            sq, bsq = r_sq.next()
                            cp("dve", rw[:, 0:n], pq[:, 0:n], [Bpq], [brw])
                            act(sq[:, 0:n], pq[:, 0:n], AF.Square, [Bpq], [bsq])
                            yield
                            mm(pss[:, 0:n], ONESb[:], sq[:, 0:n], True, True, [bsq, Bc], [Bpss])
                            yield
                            ln, bln = r_ln.next()
                            act(ln[:, 0:n], pss[:, 0:n], AF.Ln, [Bpss], [bln], scale=1.0 / 128, bias=RMS_EPS)
                            yield
                            act(ln[:, 0:n], ln[:, 0:n], AF.Exp, [bln], [bln], scale=-0.5)
                            yield
                            stt(dstT[:, p0:p0 + n], rw[:, 0:n], gcol[:, 0:1], ln[:, 0:n], ALU.mult, ALU.mult,
                                [brw, bln, Bgq], [bd])
                            if which == "k":
                                yield
                                ts("dve", nkT[:, p0:p0 + n], kT[:, p0:p0 + n], -1.0, ALU.mult, [bd], [bd])
                            yield
                    for c in range(NCH):
                        C = NM if c == 0 else 128
                        p0 = 0 if c == 0 else NM + 128 * (c - 1)
                        for k in range(KT):
                            mm(pq[0:C, 0:128], hnT[:, k, p0:p0 + C], W[:, k, 256:384], k == 0, k == KT - 1, [bW], [Bpq])
                        yield
                        cp("dve", vS[0:C, c, :], pq[0:C, 0:128], [Bpq], [Bv])
                        yield

                W0 = load_sb_w(0)
                for _ in proj(0, *W0):
                    pass
                for h in range(8):
                    H = HB[h % 2]
                    qT, kT, nkT, vS = H["qT"], H["kT"], H["nkT"], H["vS"]
                    Bq, Bk, Bv = H["Bq"], H["Bk"], H["Bv"]
                    pgen = None
                    if h + 1 < 8:
                        Wn = load_sb_w(h + 1)
                        pgen = proj(h + 1, *Wn)
                    def attn(g, L):
                        qp0 = NM + 512 * g
                        blocks = [(b_, b_ - 4 * g) for b_ in range(4 * g + 3, -1, -1)] + [(-1, -1)]
                        ZC, bZC = L["ZC"]
                        OT, bOT = L["OT"]
                        Rr, BR = L["R"]
                        nb = len(blocks)
                        mm(OT[:, :], Zb[:, :], qT[:, qp0:qp0 + 512], True, False, [Bq, Bc], [bOT], counted=False)
                        for i, (b_, r) in enumerate(blocks):
                            c0 = 128 * r if r > 0 else 0
                            n = 512 - c0
                            C = NM if b_ < 0 else 128
                            kp0 = 0 if b_ < 0 else NM + 128 * b_
                            mm(ZC[0:C, 0:n], kT[:, kp0:kp0 + C], qT[:, qp0 + c0:qp0 + 512], True, True, [Bk, Bq], [bZC])
                            yield
                            e, be = L["e"].next()
                            sp, bsp = L["sp"].next()
                            act(e[0:C, 0:n], ZC[0:C, 0:n], AF.Exp, [bZC], [be])
                            yield
                            act(sp[0:C, 0:n], e[0:C, 0:n], AF.Ln, [be], [bsp], bias=1.0)
                            yield
                            if r >= 0:
                                asel(sp[0:C, 0:128], [[1, 128]], ALU.is_gt, 0.0, -1, [bsp])
                                yield
                            mm(ZC[0:C, 0:n], TRIb[0:C, 0:C], sp[0:C, 0:n], True, False, [bsp, Bc], [bZC], counted=False)
                            cs = 128 if r >= 0 else 0
                            if n - cs > 0 and i > 0:
                                mm(ZC[0:C, cs:n], ONESb[:, 0:C], Rr[:, c0 + cs:512], False, False, [BR, Bc], [bZC], counted=False)
                            mm(ZC[0:C, 0:n], nkT[:, kp0:kp0 + C], qT[:, qp0 + c0:qp0 + 512], False, True, [Bk, Bq], [bZC])
                            yield
                            if i < nb - 1:
                                if r >= 0:
                                    cp("dve", Rr[:, c0:c0 + 128], sp[:, 0:128], [bsp], [BR])
                                    if n > 128:
                                        tt("dve", Rr[:, c0 + 128:512], Rr[:, c0 + 128:512], sp[:, 128:n], ALU.add, [bsp, BR], [BR])
                                else:
                                    tt("dve", Rr[:, :], Rr[:, :], sp[:, 0:512], ALU.add, [bsp, BR], [BR])
                            w, bw = L["w"].next()
                            act(w[0:C, 0:n], ZC[0:C, 0:n], AF.Exp, [bZC], [bw], scale=-1.0)
                            yield
                            if r >= 0:
                                asel(w[0:C, 0:128], [[1, 128]], ALU.is_gt, 0.0, -1, [bw])
                                yield
                            last = (i == nb - 1)
                            mm(OT[:, c0:512], vS[0:C, b_ + 1, :], w[0:C, 0:n], False, last, [Bv, bw], [bOT], counted=last)
                            yield
                        ot, bot = L["ot"].next()
                        cp("dve", ot[:], OT[:], [bOT], [bot])
                        yield
                        P.dma("sp", osbT[h * 128:(h + 1) * 128, 512 * g:512 * (g + 1)], ot[:], reads=[bot])

                    todo = list(range(NG - 1, -1, -1))
                    active = []
                    free_l = list(range(NL))
                    while todo or active or pgen is not None:
                        while free_l and todo:
                            li = free_l.pop(0)
                            active.append([attn(todo.pop(0), lanes[li]), li])
                        for item in list(active):
                            try:
                                next(item[0])
                            except StopIteration:
                                active.remove(item)
                                free_l.append(item[1])
                        if pgen is not None:
                            try:
                                next(pgen)
                            except StopIteration:
                                pgen = None
                P.barrier()
        else:
            P.barrier()
        mid.close()

        with contextlib.ExitStack() as ph:
            sb, ps = alloc_in(ph)
            Wg = sb("Wg", [128, KT, 2048], BF16)
            Wbd = sb("Wbd", [128, 16, D], BF16)
            Wbs = sb("Wbs", [128, 8, D], BF16)
            Wo = sb("Wo", [128, 8, D], BF16)
            BWg, BWbd, BWbs, BWo = Buf(), Buf(), Buf(), Buf()
            for c2 in range(2):
                P.dma("pool", Wg[:, :, c2 * 1024:(c2 + 1) * 1024],
                      w_in[:, OFF_GDN + c2 * 1024:OFF_GDN + (c2 + 1) * 1024].rearrange("(k p) n -> p k n", p=128), writes=[BWg])
            for c2 in range(2):
                P.dma("pool", Wbd[:, c2 * 8:(c2 + 1) * 8, :],
                      w_bdn[c2 * 1024:(c2 + 1) * 1024, :].rearrange("(k p) n -> p k n", p=128), writes=[BWbd])
            P.dma("pool", Wbs[:], w_bsb.rearrange("(k p) n -> p k n", p=128), writes=[BWbs])
            P.dma("pool", Wo[:], w_out.rearrange("(k p) n -> p k n", p=128), writes=[BWo])
            g1 = sb("g1d", [128, KT], F32)
            Bg1 = Buf()
            P.dma("sp", g1[:], g_mix.rearrange("(k p) -> p k", p=128), writes=[Bg1], allow_slow_non_contiguous=True)
            R = {"junk": Rot(sb, "junkD", [128, D], BF16, 1), "ss": Rot(sb, "ssD", [128, 1], F32, 2),
                 "xs": Rot(sb, "xsD", [128, D], BF16, 2), "pT": Rot(ps, "pTD", [128, D], BF16, 1, excl=True)}
            x4R = Rot(sb, "x4", [128, 4, D], F32, 1)
            hsR = Rot(sb, "hs", [128, KT, 512], BF16, 1)
            odR = Rot(sb, "od", [128, 16, 512], BF16, 1)
            osR = Rot(sb, "os", [128, 8, 512], BF16, 1)
            mTR = Rot(sb, "mT", [128, 8, 512], BF16, 1)
            thR = Rot(sb, "thD", [128, 512], F32, 4)
            t1R = Rot(sb, "t1D", [128, 512], F32, 4)
            pass
            hoR = Rot(sb, "ho", [128, D], F32, 2)
            pG = Rot(ps, "pG", [128, 512], F32, 7, excl=True)
            pO = pG
            for s in range(NG):
                t0 = 512 * s
                x4, bx4 = x4R.next()
                for ts_ in range(4):
                    P.dma("sp", x4[:, ts_, :], x[t0 + 128 * ts_:t0 + 128 * (ts_ + 1), :], writes=[bx4])
                od, bod = odR.next()
                osb_, bos = osR.next()
                if do_dn:
                    P.dma("sp", od[:], odnT[:, t0:t0 + 512].rearrange("(k p) t -> p k t", p=128), writes=[bod])
                else:
                    memset("dve", od[:], 0.0, [bod])
                if do_sb:
                    P.dma("sp", osb_[:], osbT[:, t0:t0 + 512].rearrange("(k p) t -> p k t", p=128), writes=[bos])
                else:
                    memset("dve", osb_[:], 0.0, [bos])
                hs, bhs = hsR.next()
                for ts_ in range(4):
                    def dst(k, ts_=ts_, hs=hs, bhs=bhs):
                        return hs[:, k, ts_ * 128:(ts_ + 1) * 128], [bhs]
                    rmsnorm_T(sb, ps, "D", 128, x4[:, ts_, :], g1, dst, R, bx4, Bg1)
                mT, bmT = mTR.next()
                for dt_ in range(8):
                    pgd_, bpgd = pG.next()
                    pgs_, bpgs = pG.next()
                    pA, bpA = pG.next()
                    pB, bpB = pG.next()
                    for k in range(KT):
                        mm(pgd_[:], Wg[:, k, dt_ * 128:(dt_ + 1) * 128], hs[:, k, :], k == 0, k == KT - 1, [BWg, bhs], [bpgd])
                    for k in range(KT):
                        mm(pgs_[:], Wg[:, k, 1024 + dt_ * 128:1024 + (dt_ + 1) * 128], hs[:, k, :], k == 0, k == KT - 1,
                           [BWg, bhs], [bpgs])
                    for k in range(16):
                        mm(pA[:], Wbd[:, k, dt_ * 128:(dt_ + 1) * 128], od[:, k, :], k == 0, k == 15, [BWbd, bod], [bpA])
                    for k in range(8):
                        mm(pB[:], Wbs[:, k, dt_ * 128:(dt_ + 1) * 128], osb_[:, k, :], k == 0, k == 7, [BWbs, bos], [bpB])
                    thd, bthd = thR.next()
                    ths, bths = thR.next()
                    act(thd[:], pgd_[:], AF.Tanh, [bpgd], [bthd], scale=0.5)
                    act(ths[:], pgs_[:], AF.Tanh, [bpgs], [bths], scale=0.5)
                    t1, bt1 = t1R.next()
                    t2, bt2 = t1R.next()
                    stt(t1[:], thd[:], 1.0, pA[:], ALU.add, ALU.mult, [bthd, bpA], [bt1])
                    stt(t2[:], ths[:], 1.0, pB[:], ALU.add, ALU.mult, [bths, bpB], [bt2])
                    tt("dve", mT[:, dt_, :], t1[:], t2[:], ALU.add, [bt1, bt2], [bmT])
                for ts_ in range(4):
                    ho, bho = hoR.next()
                    for ch in range(2):
                        po, bpo = pO.next()
                        for k in range(8):
                            mm(po[:], mT[:, k, ts_ * 128:(ts_ + 1) * 128], Wo[:, k, ch * 512:(ch + 1) * 512], k == 0, k == 7,
                               [bmT, BWo], [bpo])
                        stt(ho[:, ch * 512:(ch + 1) * 512], po[:], 0.5, x4[:, ts_, ch * 512:(ch + 1) * 512], ALU.mult, ALU.add,
                            [bpo, bx4], [bho])
                    P.dma("sp", h1[t0 + 128 * ts_:t0 + 128 * (ts_ + 1), :], ho[:], reads=[bho])
            P.barrier()

        with contextlib.ExitStack() as ph:
            sb, ps = alloc_in(ph)
            SW = 256
            Wfi = sb("Wfi", [128, KT, 2 * DFF], BF16)
            Wfo = sb("Wfo", [128, NFF, D], BF16)
            BWf = [Buf() for _ in range(4)]
            BWo2 = [Buf(), Buf()]
            for c2 in (0, 2, 1, 3):
                P.dma("pool", Wfi[:, :, c2 * 1408:(c2 + 1) * 1408],
                      w_fi[:, c2 * 1408:(c2 + 1) * 1408].rearrange("(k p) n -> p k n", p=128), writes=[BWf[c2]])
            P.dma("pool", Wfo[:, 0:11, :], w_fo[0:1408, :].rearrange("(k p) n -> p k n", p=128), writes=[BWo2[0]])
            P.dma("pool", Wfo[:, 11:22, :], w_fo[1408:2816, :].rearrange("(k p) n -> p k n", p=128), writes=[BWo2[1]])
            g2 = sb("g2", [128, KT], F32)
            Bg2 = Buf()
            P.dma("sp", g2[:], g_ffn.rearrange("(k p) -> p k", p=128), writes=[Bg2], allow_slow_non_contiguous=True)
            R = {"junk": Rot(sb, "junkE", [128, D], BF16, 1), "ss": Rot(sb, "ssE", [128, 1], F32, 2),
                 "xs": Rot(sb, "xsE", [128, D], BF16, 2), "pT": Rot(ps, "pTE", [128, D], BF16, 1, excl=True)}
            NS = SW // 128
            h4R = Rot(sb, "h4", [128, NS, D], F32, 2)
            hsR = Rot(sb, "hs2", [128, KT, SW], BF16, 2)
            aTR = Rot(sb, "aT", [128, NFF, SW], BF16, 1)
            thR = Rot(sb, "thE", [128, SW], F32, 3)
            t1R = Rot(sb, "t1E", [128, SW], F32, 3)
            hoR = Rot(sb, "ho2", [128, D], F32, 2)
            pG = Rot(ps, "pG2", [128, 512], F32, 7, excl=True)
            pO = pG
            PRO = {}

            def pro(s_):
                t0 = SW * s_
                h4, bh4 = h4R.next()
                for ts_ in range(NS):
                    P.dma("sp", h4[:, ts_, :], h1[t0 + 128 * ts_:t0 + 128 * (ts_ + 1), :], writes=[bh4])
                hs, bhs = hsR.next()
                for ts_ in range(NS):
                    def dst(k, ts_=ts_, hs=hs, bhs=bhs):
                        return hs[:, k, ts_ * 128:(ts_ + 1) * 128], [bhs]
                    yield from rmsnorm_T_gen(sb, ps, "E", 128, h4[:, ts_, :], g2, dst, R, bh4, Bg2)
                    yield
                PRO[s_] = (h4, bh4, hs, bhs)

            def main(s_):
                t0 = SW * s_
                h4, bh4, hs, bhs = PRO.pop(s_)
                aT, baT = aTR.next()
                for f in range(NFF):
                    pg, bpg = pG.next()
                    pu, bpu = pG.next()
                    for k in range(KT):
                        mm(pg[:, 0:SW], Wfi[:, k, f * 128:(f + 1) * 128], hs[:, k, :], k == 0, k == KT - 1, [BWf[f // 11], bhs], [bpg])
                    for k in range(KT):
                        mm(pu[:, 0:SW], Wfi[:, k, DFF + f * 128:DFF + (f + 1) * 128], hs[:, k, :], k == 0, k == KT - 1,
                           [BWf[2 + f // 11], bhs], [bpu])
                    th, bth = thR.next()
                    act(th[:], pg[:, 0:SW], AF.Tanh, [bpg], [bth], scale=0.5)
                    t1, bt1 = t1R.next()
                    stt(t1[:], th[:], 1.0, pg[:, 0:SW], ALU.add, ALU.mult, [bth, bpg], [bt1])
                    tt("dve", aT[:, f, :], t1[:], pu[:, 0:SW], ALU.mult, [bt1, bpu], [baT])
                    yield
                for ts_ in range(NS):
                    ho, bho = hoR.next()
                    for ch in range(2):
                        po, bpo = pO.next()
                        for f in range(NFF):
                            mm(po[:], aT[:, f, ts_ * 128:(ts_ + 1) * 128], Wfo[:, f, ch * 512:(ch + 1) * 512], f == 0, f == NFF - 1,
                               [baT, BWo2[f // 11]], [bpo])
                        stt(ho[:, ch * 512:(ch + 1) * 512], po[:], 0.5, h4[:, ts_, ch * 512:(ch + 1) * 512], ALU.mult, ALU.add,
                            [bpo, bh4], [bho])
                        yield
                    P.dma("sp", out[t0 + 128 * ts_:t0 + 128 * (ts_ + 1), :], ho[:], reads=[bho])

            NSUP = NT // SW
            for _ in pro(0):
                pass
            for s_ in range(NSUP):
                gens = [main(s_)]
                if s_ + 1 < NSUP:
                    gens.append(pro(s_ + 1))
                while gens:
                    for g_ in list(gens):
                        try:
                            next(g_)
                        except StopIteration:
                            gens.remove(g_)
            P.barrier()
        P.emit()
    return nc


_CACHE = {}


def _names():
    return dict(x="x", meta_tokens="meta", norm_mix_gain="g_mix", w_in="w_in", conv_q="conv_q", conv_k="conv_k",
                conv_v="conv_v", dn_a_log="a_log", dn_dt_bias="dt_bias", dn_out_norm_gain="g_dn",
                sb_q_norm_gain="g_sbq", sb_k_norm_gain="g_sbk", w_branch_dn="w_bdn", w_branch_sb="w_bsb",
                w_out="w_out", norm_ffn_gain="g_ffn", w_ffn_in="w_fi", w_ffn_out="w_fo")


def run(inputs, **bkw):
    xs = np.ascontiguousarray(np.asarray(inputs["x"], dtype=np.float32))
    B, NT, _ = xs.shape
    shared = {}
    for k, v in inputs.items():
        if k == "x":
            continue
        a = np.asarray(v, dtype=np.float32)
        if k != "meta_tokens":
            a = a[0]
        shared[_names()[k]] = np.ascontiguousarray(a)
    nc = build(NT, **bkw)
    in_maps = [dict(shared, x=xs[b]) for b in range(B)]
    res = run_bass_kernel_spmd(nc, in_maps, core_ids=list(range(B)))
    return res


def kernel(**inputs):
    res = run(inputs)
    return np.stack([np.asarray(r["out"], dtype=np.float32) for r in res.results], axis=0)
```

```python
import contextlib
import numpy as np
import concourse.bass as bass
import concourse.mybir as mybir
from concourse.bass_utils import run_bass_kernel_spmd

F32 = mybir.dt.float32
BF16 = mybir.dt.bfloat16
AF = mybir.ActivationFunctionType
ALU = mybir.AluOpType

D = 1024
KT = 8
NM = 16
DFF = 2816
NFF = DFF // 128
OFF_DQ, OFF_DK, OFF_DV, OFF_DZ, OFF_DA = 0, 1024, 2048, 4096, 6144
OFF_SQ, OFF_SK, OFF_SV, OFF_GDN, OFF_GSB = 6160, 7184, 8208, 9232, 10256
PW = 11280
RMS_EPS = 1e-6
L2_EPS = 1e-6


class Buf:
    __slots__ = ("name", "w", "r", "excl")

    def __init__(self, name="", excl=False):
        self.name = name
        self.w = None
        self.r = {}
        self.excl = excl


class Prog:
    ENGS = ("pe", "act", "dve", "pool", "sp")

    def __init__(self, nc, stack, ndma=8):
        self.nc = nc
        self.streams = {e: [] for e in self.ENGS}
        self.count = {e: 0 for e in self.ENGS}
        self.seen = {e: {} for e in self.ENGS}
        self.sems = {}
        for e in ("pe", "act", "dve", "pool"):
            self.sems[e] = stack.enter_context(nc.semaphore("s_" + e))
        self.dma_sems, self.dma_val, self.dma_rr = {}, {}, {}
        for q in ("sp", "pool"):
            keys = []
            for i in range(ndma):
                k = "d_%s%d" % (q, i)
                self.sems[k] = stack.enter_context(nc.semaphore(k))
                self.dma_val[k] = 0
                keys.append(k)
            self.dma_sems[q] = keys
            self.dma_rr[q] = 0

    def _deps(self, eng, reads, writes):
        toks = []
        for b in reads:
            if b.w is not None:
                toks.append(b.w)
        for b in writes:
            if b.w is not None:
                toks.append(b.w)
            toks.extend(b.r.items())
        seen = self.seen[eng]
        mx = {}
        for k, v in toks:
            if k == eng and eng == "pe":
                continue
            if seen.get(k, 0) >= v:
                continue
            if mx.get(k, 0) < v:
                mx[k] = v
        for k, v in mx.items():
            seen[k] = v
        return list(mx.items())

    def _mark(self, tok, reads, writes):
        k, v = tok
        for b in reads:
            if b.r.get(k, 0) < v:
                b.r[k] = v
        for b in writes:
            b.w = tok
            b.r = {}

    @staticmethod
    def _split(reads, writes):
        xr = [b for b in reads if b.excl]
        if xr:
            reads = [b for b in reads if not b.excl]
            writes = list(writes) + xr
        return reads, writes

    def op(self, eng, fn, reads=(), writes=(), counted=True):
        reads, writes = self._split(reads, writes)
        waits = self._deps(eng, reads, writes)
        if counted:
            self.count[eng] += 1
            tok = (eng, self.count[eng])
        else:
            tok = (eng, self.count[eng] + 1)
        self.streams[eng].append((waits, fn, (eng, 1) if counted else None))
        self._mark(tok, reads, writes)

    def dma(self, q, out, in_, reads=(), writes=(), **kw):
        keys = self.dma_sems[q]
        k = keys[self.dma_rr[q] % len(keys)]
        self.dma_rr[q] += 1
        waits = self._deps(q, reads, writes)
        prev = self.dma_val[k]
        if prev > 0 and self.seen[q].get(k, 0) < prev:
            self.seen[q][k] = prev
            waits.append((k, prev))
        self.dma_val[k] = prev + 16
        tok = (k, prev + 16)

        def fn(e, out=out, in_=in_, kw=kw):
            return e.dma_start(out=out, in_=in_, **kw)
        self.streams[q].append((waits, fn, (k, 16)))
        self._mark(tok, reads, writes)

    def barrier(self):
        allw = [(e, self.count[e]) for e in ("pe", "act", "dve", "pool") if self.count[e] > 0]
        allw += [(k, v) for k, v in self.dma_val.items() if v > 0]
        for e in self.ENGS:
            waits = []
            for k, v in allw:
                if k == e:
                    continue
                if self.seen[e].get(k, 0) < v:
                    self.seen[e][k] = v
                    waits.append((k, v))
            self.streams[e].append((waits, None, None))

    def emit(self):
        nc = self.nc
        with nc.Block() as block:
            def run(eng):
                def body(e):
                    for waits, fn, inc in self.streams[eng]:
                        for k, v in waits:
                            e.wait_ge(self.sems[k], v)
                        if fn is None:
                            continue
                        ins = fn(e)
                        if inc is not None:
                            ins.then_inc(self.sems[inc[0]], inc[1])
                return body
            block.tensor(run("pe"))
            block.scalar(run("act"))
            block.vector(run("dve"))
            block.gpsimd(run("pool"))
            block.sync(run("sp"))


class Rot:
    def __init__(self, alloc, name, shape, dt, n, excl=False):
        self.items = [(alloc("%s%d" % (name, i), shape, dt), Buf("%s%d" % (name, i), excl)) for i in range(n)]
        self.i = 0

    def next(self):
        it = self.items[self.i % len(self.items)]
        self.i += 1
        return it


def build(NT, do_dn=True, do_sb=True, dbg=False):
    NPOS = NM + NT
    NTT = NT // 128
    NCH = 1 + NTT
    NG = NT // 512
    nc = bass.Bass("TRN2", target_bir_lowering=False)

    def dram(name, shape, dt=F32, kind="ExternalInput"):
        return nc.dram_tensor(name, shape, dt, kind=kind).ap()

    x = dram("x", [NT, D])
    meta = dram("meta", [NM, D])
    g_mix = dram("g_mix", [D])
    w_in = dram("w_in", [D, PW])
    conv_q = dram("conv_q", [4, 1024])
    conv_k = dram("conv_k", [4, 1024])
    conv_v = dram("conv_v", [4, 2048])
    a_log = dram("a_log", [8])
    dt_bias = dram("dt_bias", [8])
    g_dn = dram("g_dn", [256])
    g_sbq = dram("g_sbq", [128])
    g_sbk = dram("g_sbk", [128])
    w_bdn = dram("w_bdn", [2048, D])
    w_bsb = dram("w_bsb", [1024, D])
    w_out = dram("w_out", [D, D])
    g_ffn = dram("g_ffn", [D])
    w_fi = dram("w_fi", [D, 2 * DFF])
    w_fo = dram("w_fo", [DFF, D])
    out = dram("out", [NT, D], F32, kind="ExternalOutput")
    okind = "ExternalOutput" if dbg else "Internal"
    odnT = dram("odnT", [2048, NT], BF16, kind=okind)
    osbT = dram("osbT", [1024, NT], BF16, kind=okind)
    h1 = dram("h1", [NT, D], F32, kind=okind)

    with contextlib.ExitStack() as glob:
        P = Prog(nc, glob)

        def alloc_in(st):
            def sb(name, shape, dt):
                return st.enter_context(nc.sbuf_tensor(name, shape, dt))

            def ps(name, shape, dt):
                return st.enter_context(nc.psum_tensor(name, shape, dt))
            return sb, ps

        gsb, _ = alloc_in(glob)

        def mm(o, lhsT, rhs, start, stop, rd, wr, counted=None, skip=False):
            if counted is None:
                counted = stop
            P.op("pe", lambda e: e.matmul(o, lhsT=lhsT, rhs=rhs, start=start, stop=stop, skip_group_check=skip),
                 rd, wr, counted)

        def tr(o, in_, ident, rd, wr, counted=True):
            P.op("pe", lambda e: e.transpose(out=o, in_=in_, identity=ident), rd, wr, counted)

        def act(o, in_, func, rd, wr, scale=1.0, bias=None, accum=None):
            def fn(e):
                kw = {}
                if bias is not None:
                    kw["bias"] = bias
                if accum is not None:
                    kw["accum_out"] = accum
                return e.activation(out=o, in_=in_, func=func, scale=scale, **kw)
            P.op("act", fn, rd, wr)

        def ts(eng, o, in0, s1, op0, rd, wr, s2=None, op1=None):
            def fn(e):
                if op1 is None:
                    return e.tensor_scalar(out=o, in0=in0, scalar1=s1, scalar2=None, op0=op0)
                return e.tensor_scalar(out=o, in0=in0, scalar1=s1, scalar2=s2, op0=op0, op1=op1)
            P.op(eng, fn, rd, wr)

        def tt(eng, o, in0, in1, op, rd, wr):
            P.op(eng, lambda e: e.tensor_tensor(out=o, in0=in0, in1=in1, op=op), rd, wr)

        def stt(o, in0, scalar, in1, op0, op1, rd, wr):
            P.op("dve", lambda e: e.scalar_tensor_tensor(out=o, in0=in0, scalar=scalar, in1=in1, op0=op0, op1=op1), rd, wr)

        def cp(eng, o, in_, rd, wr):
            if eng == "act":
                act(o, in_, AF.Copy, rd, wr)
            else:
                P.op(eng, lambda e: e.tensor_copy(out=o, in_=in_), rd, wr)

        def memset(eng, o, val, wr):
            P.op(eng, lambda e: e.memset(o, val), (), wr)

        def asel(o, pattern, cmp, fill, cm, rd_wr, base=0):
            P.op("pool", lambda e: e.affine_select(out=o, in_=o, pattern=pattern, compare_op=cmp, fill=fill,
                                                   base=base, channel_multiplier=cm), rd_wr, rd_wr)

        identf = gsb("identf", [128, 128], F32)
        identb = gsb("identb", [128, 128], BF16)
        UT = gsb("UT", [128, 128], F32)
        SL = gsb("SL", [128, 128], F32)
        MASKS = gsb("MASKS", [128, 256], F32)
        ONESf = gsb("ONESf", [128, 128], F32)
        ONESb = gsb("ONESb", [128, 128], BF16)
        TRIb = gsb("TRIb", [128, 128], BF16)
        Zb = gsb("Zb", [128, 128], BF16)
        mhalf = gsb("mhalf", [128, 2], F32)
        Bc = Buf("consts")
        memset("pool", identf[:], 0.0, [Bc])
        asel(identf[:], [[-1, 128]], ALU.not_equal, 1.0, 1, [Bc])
        memset("pool", UT[:], 1.0, [Bc])
        asel(UT[:], [[1, 128]], ALU.is_ge, 0.0, -1, [Bc])
        memset("pool", SL[:], 1.0, [Bc])
        asel(SL[:], [[-1, 128]], ALU.is_gt, 0.0, 1, [Bc])
        memset("pool", ONESf[:], 1.0, [Bc])
        memset("pool", Zb[:], 0.0, [Bc])
        memset("pool", mhalf[:], -0.5, [Bc])
        cp("dve", identb[:], identf[:], [Bc], [Bc])
        cp("dve", ONESb[:], ONESf[:], [Bc], [Bc])
        cp("dve", MASKS[:, 0:128], SL[:], [Bc], [Bc])
        cp("dve", MASKS[:, 128:256], UT[:], [Bc], [Bc])
        trif = gsb("trif", [128, 128], F32)
        memset("pool", trif[:], 1.0, [Bc])
        asel(trif[:], [[-1, 128]], ALU.is_ge, 0.0, 1, [Bc])
        cp("dve", TRIb[:], trif[:], [Bc], [Bc])

        UTb = gsb("UTb", [128, 128], BF16)
        SLb = gsb("SLb", [128, 128], BF16)
        SLx = gsb("SLx", [128, 129], F32)
        UTx = gsb("UTx", [128, 129], F32)
        MASKX = gsb("MASKX", [128, 258], F32)
        cp("dve", UTb[:], UT[:], [Bc], [Bc])
        cp("dve", SLb[:], SL[:], [Bc], [Bc])
        memset("pool", SLx[:], 1.0, [Bc])
        memset("pool", UTx[:], 1.0, [Bc])
        memset("pool", MASKX[:], 1.0, [Bc])
        cp("dve", SLx[:, 0:128], SL[:], [Bc], [Bc])
        cp("dve", UTx[:, 0:128], UT[:], [Bc], [Bc])
        ts("dve", MASKX[:, 0:128], SL[:], -1.0, ALU.mult, [Bc], [Bc])
        SUX = gsb("SUX", [128, 2, 130], F32)
        memset("pool", SUX[:], 0.0, [Bc])
        cp("dve", SUX[:, 0, 0:129], SLx[:], [Bc], [Bc])
        cp("dve", SUX[:, 1, 0:129], UTx[:], [Bc], [Bc])
        cp("dve", MASKX[:, 129:257], UT[:], [Bc], [Bc])
        BLK = gsb("BLK", [128, 6, 128], F32)
        with contextlib.ExitStack() as tmpst:
            tsb, tps = alloc_in(tmpst)
            Ab = tsb("Ab", [128, 128], F32)
            Eb = tsb("Eb", [128, 5, 128], F32)
            pE = tps("pE", [128, 512], F32)
            BA, BE, BpE = Buf(), Buf(), Buf("pE", True)
            for li, b in enumerate((4, 8, 16, 32, 64)):
                memset("pool", Ab[:], 1.0, [BA])
                asel(Ab[:], [[1, 128]], ALU.is_ge, 0.0, -b, [BA])
                asel(Ab[:], [[-1, 128]], ALU.is_ge, 0.0, b, [BA], base=b - 1)
                nr = 128 // b
                mm(pE[:, 0:128], Ab[0:nr, :], Ab[0:nr, :], True, True, [BA], [BpE])
                cp("dve", Eb[:, li, :], pE[:, 0:128], [BpE], [BE])
            cp("dve", BLK[:, 0, :], Eb[:, 0, :], [BE], [Bc])
            for li in range(1, 5):
                tt("dve", BLK[:, li, :], Eb[:, li, :], Eb[:, li - 1, :], ALU.subtract, [BE], [Bc])
            tt("dve", BLK[:, 5, :], ONESf[:], Eb[:, 4, :], ALU.subtract, [BE, Bc], [Bc])
            P.barrier()
        mid = contextlib.ExitStack()
        msb, _ = alloc_in(mid)
        hnT = msb("hnT", [128, KT, NPOS], BF16)
        ab_all = msb("ab_all", [128, NCH, 16], F32)
        P.barrier()

        def rmsnorm_T_gen(sbx, psx, tag, n_rows, xrow_ap, gcol, dst_fn, R, Bx, Bg):
            junk, bj = R["junk"].next()
            ss, bs = R["ss"].next()
            act(junk[0:n_rows, :], xrow_ap, AF.Square, [Bx], [bj, bs], accum=ss[0:n_rows, 0:1])
            yield
            ts("pool", ss[0:n_rows, 0:1], ss[0:n_rows, 0:1], 1.0 / D, ALU.mult, [bs], [bs], s2=RMS_EPS, op1=ALU.add)
            yield
            tt("pool", ss[0:n_rows, 0:1], ss[0:n_rows, 0:1], mhalf[0:n_rows, 0:1], ALU.pow, [bs], [bs])
            yield
            xs, bxs = R["xs"].next()
            ts("dve", xs[0:n_rows, :], xrow_ap, ss[0:n_rows, 0:1], ALU.mult, [Bx, bs], [bxs])
            yield
            pT, bp = R["pT"].next()
            for k in range(KT):
                tr(pT[:, k * 128:k * 128 + n_rows], xs[0:n_rows, k * 128:(k + 1) * 128], identb[0:n_rows, 0:n_rows],
                   [bxs], [bp], counted=(k == KT - 1))
            yield
            for k in range(KT):
                o_ap, wr = dst_fn(k)
                ts("dve", o_ap, pT[:, k * 128:k * 128 + n_rows], gcol[:, k:k + 1], ALU.mult, [bp, Bg], wr)

        def rmsnorm_T(*a_, **k_):
            for _ in rmsnorm_T_gen(*a_, **k_):
                pass

        with contextlib.ExitStack() as ph:
            sb, ps = alloc_in(ph)
            g1 = sb("g1", [128, KT], F32)
            Bg1 = Buf()
            P.dma("sp", g1[:], g_mix.rearrange("(k p) -> p k", p=128), writes=[Bg1], allow_slow_non_contiguous=True)
            wab = sb("wab", [128, KT, 16], BF16)
            Bwab = Buf()
            P.dma("pool", wab[:], w_in[:, OFF_DA:OFF_DA + 16].rearrange("(k p) n -> p k n", p=128), writes=[Bwab])
            alb = sb("alb", [128, 8], F32)
            dtb = sb("dtb", [128, 8], F32)
            Bal = Buf()
            P.dma("sp", alb[:], a_log.partition_broadcast(128), writes=[Bal])
            P.dma("sp", dtb[:], dt_bias.partition_broadcast(128), writes=[Bal])
            act(alb[:], alb[:], AF.Exp, [Bal], [Bal])
            ts("dve", alb[:], alb[:], -1.0, ALU.mult, [Bal], [Bal])
            R = {"junk": Rot(sb, "junkA", [128, D], BF16, 1), "ss": Rot(sb, "ssA", [128, 1], F32, 4),
                 "xs": Rot(sb, "xsA", [128, D], BF16, 4), "pT": Rot(ps, "pTA", [128, D], BF16, 3, excl=True)}
            xR = Rot(sb, "xA", [128, D], F32, 4)
            pab = Rot(ps, "pab", [128, 512], F32, 3, excl=True)
            tmpR = Rot(sb, "tmpA", [128, 16], F32, 4)
            def tileA(c):
                n = NM if c == 0 else 128
                p0 = 0 if c == 0 else NM + 128 * (c - 1)
                xt, bx = xR.next()
                src = meta if c == 0 else x[128 * (c - 1):128 * c, :]
                P.dma("sp", xt[0:n, :], src, writes=[bx])
                bh = Buf()

                def dst(k, p0=p0, n=n, bh=bh):
                    return hnT[:, k, p0:p0 + n], [bh]
                yield from rmsnorm_T_gen(sb, ps, "A", n, xt[0:n, :], g1, dst, R, bx, Bg1)
                yield
                pa, bpa = pab.next()
                for k in range(KT):
                    mm(pa[0:n, 0:16], hnT[:, k, p0:p0 + n], wab[:, k, :], k == 0, k == KT - 1, [bh, Bwab], [bpa])
                yield
                t, bt = tmpR.next()
                tt("dve", t[0:n, 0:8], pa[0:n, 0:8], dtb[0:n, :], ALU.add, [bpa, Bal], [bt])
                yield
                act(t[0:n, 0:8], t[0:n, 0:8], AF.Exp, [bt], [bt])
                act(t[0:n, 8:16], pa[0:n, 8:16], AF.Exp, [bpa, bt], [bt], scale=-1.0)
                yield
                act(t[0:n, 0:8], t[0:n, 0:8], AF.Ln, [bt], [bt], bias=1.0)
                ts("dve", t[0:n, 8:16], t[0:n, 8:16], 1.0, ALU.add, [bt], [bt])
                yield
                tt("dve", ab_all[0:n, c, 0:8], t[0:n, 0:8], alb[0:n, :], ALU.mult, [bt, Bal], [bt])
                P.op("dve", lambda e, n=n, t=t, c=c: e.reciprocal(out=ab_all[0:n, c, 8:16], in_=t[0:n, 8:16]), [bt], [bt])

            NLA = 3
            todoA = list(range(NCH))
            actA = []
            while todoA or actA:
                while len(actA) < NLA and todoA:
                    actA.append(tileA(todoA.pop(0)))
                for g_ in list(actA):
                    try:
                        next(g_)
                    except StopIteration:
                        actA.remove(g_)
            P.barrier()

        if do_dn:
            with contextlib.ExitStack() as ph:
                sb, ps = alloc_in(ph)
                G = 6
                NRC = G + 2
                WhR = Rot(sb, "Wh", [128, KT, 768], BF16, 2)
                cw = sb("cw", [128, 16], F32)
                dg = sb("dg", [128, 16, 128], BF16)
                Bcw, Bdg = Buf(), Buf()
                gz = sb("gz", [128, 256], F32)
                Bgz = Buf()
                P.dma("sp", gz[:], g_dn.partition_broadcast(128), writes=[Bgz])
                ts("dve", gz[:], gz[:], 0.5, ALU.mult, [Bgz], [Bgz])
                S_f = Rot(sb, "Sf", [128, 256], F32, 2)
                S_b = Rot(sb, "Sb", [128, 256], BF16, 2)
                ostg = Rot(sb, "ostg", [128, 2, 512], BF16, 2)
                slotP = [ps("slotP%d" % i, [128, 512], F32) for i in range(G)]
                slotB = [Buf("slotP%d" % i, True) for i in range(G)]
                pS1R = Rot(ps, "pS1", [128, 512], F32, 1, excl=True)
                pS2, BS2 = ps("pS2", [128, 512], F32), Buf("pS2", True)
                pS2b = pS2[:].bitcast(BF16)
                SL_ = []
                for i in range(G):
                    d_ = {}
                    for (nm, shp, dt_) in (("s2", [128, 512], F32), ("sc", [128, 24], F32), ("r4", [128, 4, 130], BF16), ("ahl", [128, 2], BF16),
                                           ("EX", [128, 258], F32), ("tokp", [128, 4, 128], BF16), ("bv", [128, 256], BF16),
                                           ("TQ", [128, 2, 128], BF16), ("M", [128, 128], F32), ("Mmb", [128, 6, 128], BF16),
                                           ("N0b", [128, 128], BF16), ("M1b", [128, 128], BF16),
                                           ("Q", [128, 128], BF16), ("X", [128, 128], BF16),
                                           ("Yb0", [128, 128], BF16), ("Yb1", [128, 128], BF16),
                                           ("rawc", [128, 4, 132], BF16), ("junk2", [128, 256], BF16)):
                        d_[nm] = (sb("%s_%d" % (nm, i), shp, dt_), Buf("%s_%d" % (nm, i)))
                    d_["r4b"] = [Buf(), Buf()]
                    d_["junk2b"] = [Buf(), Buf()]
                    d_["sc2b"] = Buf()
                    d_["tokb"] = [Buf() for _ in range(4)]
                    d_["TQb"] = [Buf(), Buf()]
                    SL_.append(d_)
                junk, bjk = sb("junkB", [128, 256], BF16), Buf()
                c_kd = Rot(sb, "c_kd", [128, 128], BF16, NRC)
                c_qdT = Rot(sb, "c_qdT", [128, 128], BF16, NRC)
                c_at = Rot(sb, "c_at", [128, 128], BF16, NRC)
                c_u = Rot(sb, "c_u", [128, 256], F32, NRC)
                c_wT = Rot(sb, "c_wT", [128, 128], BF16, NRC)
                c_zsg = Rot(sb, "c_zsg", [128, 256], BF16, NRC)
                c_ex3 = Rot(sb, "c_ex3", [128, 4], F32, NRC)
                r_vn = Rot(sb, "vn", [128, 256], BF16, 2)
                r_so = Rot(sb, "so", [128, 1], F32, 2)
                r_oo = Rot(sb, "oo", [128, 256], BF16, 2)

                def load_head_w(h):
                    W, bW = WhR.next()
                    for (o0, src0, wdt) in ((0, OFF_DQ + h * 128, 128), (128, OFF_DK + h * 128, 128),
                                            (256, OFF_DV + h * 256, 256), (512, OFF_DZ + h * 256, 256)):
                        P.dma("pool", W[:, :, o0:o0 + wdt], w_in[:, src0:src0 + wdt].rearrange("(k p) n -> p k n", p=128),
                              writes=[bW])
                    return W, bW

                ready = {}

                def prep(h, c, si, W, bW, key, dgt):
                    dg, Bdg = dgt
                    C = NM if c == 0 else 128
                    p0 = 0 if c == 0 else NM + 128 * (c - 1)
                    Pp, BP = slotP[si], slotB[si]
                    Ppb = Pp[:].bitcast(BF16)
                    d_ = SL_[si]
                    s2, bs2 = d_["s2"]
                    sc, bsc = d_["sc"]
                    r4, _ = d_["r4"]
                    br4s = d_["r4b"]
                    junk2, _ = d_["junk2"]
                    bj2a, bj2b = d_["junk2b"]
                    bsc2 = d_["sc2b"]
                    btoks = d_["tokb"]
                    bTQq, bTQk = d_["TQb"]
                    ahl, bahl = d_["ahl"]
                    Mmb, bMmb = d_["Mmb"]
                    EX, bEX = d_["EX"]
                    tokp, btok = d_["tokp"]
                    bv, bbv = d_["bv"]
                    TQ, bTQ = d_["TQ"]
                    M, bM = d_["M"]
                    N0b, bN0b = d_["N0b"]
                    M1b, bM1b = d_["M1b"]
                    Q, bQ = d_["Q"]
                    X, bX = d_["X"]
                    Ybs = [d_["Yb0"], d_["Yb1"]]
                    rawc, Braw = d_["rawc"]
                    zt, bzt = d_["s2"][0][:, 0:256], d_["s2"][1]
                    kd, bkd = c_kd.next()
                    qdT, bqdT = c_qdT.next()
                    at, bat = c_at.next()
                    u, bu = c_u.next()
                    wT, bwT = c_wT.next()
                    zsg, bzsg = c_zsg.next()
                    ex3, bex3 = c_ex3.next()
                    if c == 0:
                        memset("pool", rawc[:, :, 0:3], 0.0, [Braw])
                        hs0, nr_, ro = 0, C, 3
                    else:
                        hs0, nr_, ro = p0 - 3, C + 3, 0
                    for (cts, eng_) in (((0, 1, 2), "act"), ((3,), "dve")):
                        for i3, ct in enumerate(cts):
                            for k in range(KT):
                                mm(Pp[:, i3 * nr_:(i3 + 1) * nr_], W[:, k, ct * 128:(ct + 1) * 128], hnT[:, k, hs0:hs0 + nr_],
                                   k == 0, k == KT - 1, [bW], [BP])
                        zdo = (c > 0 and cts == (3,))
                        if zdo:
                            for k in range(KT):
                                mm(Pp[0:C, 192:448], hnT[:, k, p0:p0 + C], W[:, k, 512:768], k == 0, k == KT - 1, [bW], [BP])
                        yield
                        for i3, ct in enumerate(cts):
                            cp(eng_, rawc[:, ct, ro:ro + nr_], Pp[:, i3 * nr_:(i3 + 1) * nr_], [BP], [Braw])
                        if zdo:
                            act(zt[0:C, :], Pp[0:C, 192:448], AF.Tanh, [BP], [bzt], scale=0.5)
                        yield
                        if zdo:
                            stt(zt[0:C, :], zt[0:C, :], 1.0, Pp[0:C, 192:448], ALU.add, ALU.mult, [bzt, BP], [bzt])
                            yield
                            tt("dve", zsg[0:C, :], zt[0:C, :], gz[0:C, :], ALU.mult, [bzt, Bgz], [bzsg])
                    for (ct, o0) in ((0, 0), (1, 128), (2, 256), (3, 384)):
                        if c == 0 and ct == 0:
                            continue
                        for j in range(4):
                            mm(Pp[0:C, o0:o0 + 128], rawc[:, ct, j:j + C], dg[:, ct * 4 + j, :], j == 0, j == 3,
                               [Braw, Bdg], [BP])
                    yield
                    c0 = 128 if c == 0 else 0
                    act(s2[0:C, c0:512], Pp[0:C, c0:512], AF.Tanh, [BP], [bs2], scale=0.5)
                    yield
                    stt(s2[0:C, c0:512], s2[0:C, c0:512], 1.0, Pp[0:C, c0:512], ALU.add, ALU.mult, [bs2, BP], [bs2])
                    yield
                    if c > 0:
                        act(junk2[0:C, 0:128], s2[0:C, 0:128], AF.Square, [bs2], [bj2a, bsc], accum=sc[0:C, 0:1])
                    else:
                        memset("dve", sc[0:C, 0:1], 1.0, [bsc])
                    act(junk2[0:C, 128:256], s2[0:C, 128:256], AF.Square, [bs2], [bj2b, bsc], accum=sc[0:C, 1:2])
                    cp("dve", ahl[0:C, 0:1], ab_all[0:C, c, h:h + 1], [], [bahl])
                    yield
                    ts("pool", sc[0:C, 0:2], sc[0:C, 0:2], 4.0 * L2_EPS, ALU.add, [bsc], [bsc])
                    tt("dve", ahl[0:C, 1:2], ab_all[0:C, c, h:h + 1], ahl[0:C, 0:1], ALU.subtract, [bahl], [bahl])
                    yield
                    tt("pool", sc[0:C, 0:2], sc[0:C, 0:2], mhalf[0:C, 0:2], ALU.pow, [bsc], [bsc])
                    a_col = ab_all[0:C, c, h:h + 1]
                    b_col = ab_all[0:C, c, 8 + h:9 + h]
                    for i2 in range(2):
                        ts("dve", r4[0:C, 2 * i2:2 * i2 + 2, :], SUX[0:C, :, :], ahl[0:C, i2:i2 + 1], ALU.mult, [Bc, bahl],
                           [br4s[i2]])
                    yield
                    mm(Pp[0:C, 0:129], UTb[0:C, 0:C], r4[0:C, 0, 0:129], True, False, [br4s[0], Bc], [BP], counted=False)
                    mm(Pp[0:C, 0:129], UTb[0:C, 0:C], r4[0:C, 2, 0:129], False, True, [br4s[1], Bc], [BP])
                    mm(Pp[0:C, 129:258], SLb[0:C, 0:C], r4[0:C, 1, 0:129], True, False, [br4s[0], Bc], [BP], counted=False)
                    mm(Pp[0:C, 129:258], SLb[0:C, 0:C], r4[0:C, 3, 0:129], False, True, [br4s[1], Bc], [BP])
                    mm(Pp[:, 258:259], ONESb[0:C, :], ahl[0:C, 0:1], True, False, [bahl, Bc], [BP], counted=False)
                    mm(Pp[:, 258:259], ONESb[0:C, :], ahl[0:C, 1:2], False, True, [bahl, Bc], [BP])
                    yield
                    act(EX[0:C, :], Pp[0:C, 0:258], AF.Exp, [BP], [bEX])
                    act(ex3[:, 0:1], Pp[:, 258:259], AF.Exp, [BP], [bex3])
                    yield
                    tt("dve", EX[0:C, :], EX[0:C, :], MASKX[0:C, :], ALU.mult, [bEX, Bc], [bEX])
                    yield
                    tt("dve", sc[0:C, 2:3], b_col, EX[0:C, 128:129], ALU.mult, [bEX], [bsc2])
                    if c > 0:
                        ts("dve", tokp[0:C, 0, :], s2[0:C, 0:128], sc[0:C, 0:1], ALU.mult, [bs2, bsc], [btoks[0]])
                        ts("dve", tokp[0:C, 1, :], s2[0:C, 0:128], sc[0:C, 0:1], ALU.mult, [bs2, bsc, bEX], [btoks[1]],
                           s2=EX[0:C, 128:129], op1=ALU.mult)
                    ts("dve", tokp[0:C, 2, :], s2[0:C, 128:256], sc[0:C, 1:2], ALU.mult, [bs2, bsc], [btoks[2]])
                    ts("dve", tokp[0:C, 3, :], s2[0:C, 128:256], sc[0:C, 1:2], ALU.mult, [bs2, bsc, bsc2], [btoks[3]],
                       s2=sc[0:C, 2:3], op1=ALU.mult)
                    ts("pool", kd[0:C, :], s2[0:C, 128:256], sc[0:C, 1:2], ALU.mult, [bs2, bsc, bEX], [bkd],
                       s2=EX[0:C, 257:258], op1=ALU.mult)
                    ts("pool", bv[0:C, :], s2[0:C, 256:512], b_col, ALU.mult, [bs2], [bbv], s2=0.5, op1=ALU.mult)
                    yield
                    first = 2 if c == 0 else 0
                    for i in range(first, 3):
                        tr(Ppb[:, i * 128:i * 128 + C], tokp[0:C, i, :], identb[0:C, 0:C], [btoks[i], Bc], [BP], counted=(i == 2))
                    yield
                    if c > 0:
                        cp("act", TQ[:, 0, 0:C], Ppb[:, 0:C], [BP], [bTQq])
                        cp("dve", qdT[:, 0:C], Ppb[:, 128:128 + C], [BP], [bqdT])
                    cp("act", TQ[:, 1, 0:C], Ppb[:, 256:256 + C], [BP], [bTQk])
                    qT, kT = TQ[:, 0, 0:C], TQ[:, 1, 0:C]
                    yield
                    mm(Pp[0:C, 256:256 + C], kT, kT, True, True, [bTQk], [BP])
                    if c > 0:
                        mm(Pp[0:C, 384:384 + C], kT, qT, True, True, [bTQk, bTQq], [BP])
                    yield
                    stt(M[0:C, 0:C], Pp[0:C, 256:256 + C], b_col, EX[0:C, 0:C], ALU.mult, ALU.mult, [BP, bEX], [bM])
                    if c > 0:
                        tt("dve", at[0:C, 0:C], Pp[0:C, 384:384 + C], EX[0:C, 129:129 + C], ALU.mult, [BP, bEX], [bat])
                    yield
                    nlev = 2 if c == 0 else 5
                    tt("dve", Mmb[0:C, 0:nlev + 1, 0:C], M[0:C, 0:C].unsqueeze(1).broadcast_to([C, nlev + 1, C]),
                       BLK[0:C, 0:nlev + 1, 0:C], ALU.mult, [bM, Bc], [bMmb])
                    yield
                    tr(Ppb[0:C, 0:C], Mmb[0:C, 0, 0:C], identb[0:C, 0:C], [bMmb, Bc], [BP])
                    yield
                    cp("act", N0b[0:C, 0:C], Ppb[0:C, 0:C], [BP], [bN0b])
                    bi = 0
                    Yb, bYb = Ybs[bi]
                    tt("dve", Yb[0:C, 0:C], Ppb[0:C, 0:C], identb[0:C, 0:C], ALU.add, [BP, Bc], [bYb])
                    yield
                    mm(Pp[0:C, 128:128 + C], N0b[0:C, 0:C], Mmb[0:C, 0, 0:C], True, True, [bN0b, bMmb], [BP])
                    yield
                    cp("act", M1b[0:C, 0:C], Pp[0:C, 128:128 + C], [BP], [bM1b])
                    yield
                    mm(Pp[0:C, 256:256 + C], M1b[0:C, 0:C], Yb[0:C, 0:C], True, True, [bM1b, bYb], [BP])
                    yield
                    bi ^= 1
                    Yb2, bYb2 = Ybs[bi]
                    tt("dve", Yb2[0:C, 0:C], Pp[0:C, 256:256 + C], Yb[0:C, 0:C], ALU.add, [BP, bYb], [bYb2])
                    Yb, bYb = Yb2, bYb2
                    yield
                    for l in range(nlev):
                        mm(Pp[0:C, 256:256 + C], Mmb[0:C, 1 + l, 0:C], Yb[0:C, 0:C], True, True, [bMmb, bYb], [BP])
                        tr(Ppb[0:C, 768:768 + C], Yb[0:C, 0:C], identb[0:C, 0:C], [bYb, Bc], [BP])
                        yield
                        cp("act", Q[0:C, 0:C], Pp[0:C, 256:256 + C], [BP], [bQ])
                        cp("dve", X[0:C, 0:C], Ppb[0:C, 768:768 + C], [BP], [bX])
                        yield
                        mm(Pp[0:C, 0:C], identb[0:C, 0:C], Yb[0:C, 0:C], True, False, [bYb, Bc], [BP], counted=False)
                        mm(Pp[0:C, 0:C], X[0:C, 0:C], Q[0:C, 0:C], False, True, [bX, bQ], [BP])
                        yield
                        bi ^= 1
                        Yb2, bYb2 = Ybs[bi]
                        cp("act" if l % 2 == 0 else "dve", Yb2[0:C, 0:C], Pp[0:C, 0:C], [BP], [bYb2])
                        Yb, bYb = Yb2, bYb2
                        yield
                    mm(Pp[0:C, 0:256], Yb[0:C, 0:C], bv[0:C, :], True, True, [bYb, bbv], [BP])
                    mm(Pp[:, 256:256 + C], tokp[0:C, 3, :], Yb[0:C, 0:C], True, True, [bYb, btoks[3]], [BP])
                    yield
                    cp("act", u[0:C, :], Pp[0:C, 0:256], [BP], [bu])
                    cp("dve", wT[:, 0:C], Pp[:, 256:256 + C], [BP], [bwT])
                    yield
                    ready[key] = dict(kd=(kd, bkd), qdT=(qdT, bqdT), at=(at, bat), u=(u, bu), wT=(wT, bwT),
                                    zsg=(zsg, bzsg), ex3=(ex3, bex3))

                st_ = {}

                def scan(h, c, r_):
                    C = NM if c == 0 else 128
                    kd, bkd = r_["kd"]
                    qdT, bqdT = r_["qdT"]
                    at, bat = r_["at"]
                    u, bu = r_["u"]
                    wT, bwT = r_["wT"]
                    zsg, bzsg = r_["zsg"]
                    ex3, bex3 = r_["ex3"]
                    if c == 0:
                        Sf, bSf = S_f.next()
                        Sb, bSb = S_b.next()
                        memset("dve", Sf[:], 0.0, [bSf])
                        memset("dve", Sb[:], 0.0, [bSb])
                        st_["S"] = (Sf, bSf, Sb, bSb)
                    Sf, bSf, Sb, bSb = st_["S"]
                    pS1, BS1 = pS1R.next()
                    mm(pS1[0:C, 0:256], wT[:, 0:C], Sb[:], True, True, [bwT, bSb], [BS1])
                    yield
                    vn, bvn = r_vn.next()
                    tt("dve", vn[0:C, :], u[0:C, :], pS1[0:C, 0:256], ALU.subtract, [bu, BS1], [bvn])
                    yield
                    if c > 0:
                        mm(pS1[0:C, 256:512], qdT[:, 0:C], Sb[:], True, False, [bqdT, bSb], [BS1], counted=False)
                        mm(pS1[0:C, 256:512], at[0:C, 0:C], vn[0:C, :], False, True, [bat, bvn], [BS1])
                    mm(pS2[:, 0:256], kd[0:C, :], vn[0:C, :], True, True, [bkd, bvn], [BS2])
                    yield
                    Sf2, bSf2 = S_f.next()
                    stt(Sf2[:], Sf[:], ex3[:, 0:1], pS2[:, 0:256], ALU.mult, ALU.add, [bSf, bex3, BS2], [bSf2])
                    yield
                    Sb2, bSb2 = S_b.next()
                    cp("act", Sb2[:], Sf2[:], [bSf2], [bSb2])
                    st_["S"] = (Sf2, bSf2, Sb2, bSb2)
                    st_["state_done"] = True
                    if c > 0:
                        so, bso = r_so.next()
                        act(junk[0:C, :], pS1[0:C, 256:512], AF.Square, [BS1], [bjk, bso], accum=so[0:C, 0:1])
                        yield
                        ts("pool", so[0:C, 0:1], so[0:C, 0:1], 1.0 / 256, ALU.mult, [bso], [bso], s2=128.0 * RMS_EPS, op1=ALU.add)
                        tt("pool", so[0:C, 0:1], so[0:C, 0:1], mhalf[0:C, 0:1], ALU.pow, [bso], [bso])
                        yield
                        oo, boo = r_oo.next()
                        stt(oo[0:C, :], pS1[0:C, 256:512], so[0:C, 0:1], zsg[0:C, :], ALU.mult, ALU.mult,
                            [BS1, bso, bzsg], [boo])
                        yield
                        q4 = (c - 1) % 4
                        if q4 == 0:
                            st_["stg"] = ostg.next()
                        stg, bstg = st_["stg"]
                        for v in range(2):
                            tr(pS2b[:, 512 + v * 128:512 + v * 128 + C], oo[0:C, v * 128:(v + 1) * 128], identb[0:C, 0:C],
                               [boo, Bc], [BS2], counted=(v == 1))
                        yield
                        for v in range(2):
                            cp("act", stg[:, v, q4 * 128:(q4 + 1) * 128], pS2b[:, 512 + v * 128:512 + v * 128 + C], [BS2], [bstg])
                        if q4 == 3:
                            t0 = 128 * (c - 4)
                            P.dma("sp", odnT[h * 256:(h + 1) * 256, t0:t0 + 512].rearrange("(v p) t -> p v t", p=128),
                                  stg[:], reads=[bstg])

                dgs = [(dg, Bdg), (sb("dg_b", [128, 16, 128], BF16), Buf())]
                cws = [(cw, Bcw), (sb("cw_b", [128, 16], F32), Buf())]

                def head_setup(h):
                    cw_, Bcw_ = cws[h % 2]
                    dg_, Bdg_ = dgs[h % 2]
                    P.dma("sp", cw_[:, 0:4], conv_q[:, h * 128:(h + 1) * 128].rearrange("j c -> c j"), writes=[Bcw_],
                          allow_slow_non_contiguous=True)
                    P.dma("sp", cw_[:, 4:8], conv_k[:, h * 128:(h + 1) * 128].rearrange("j c -> c j"), writes=[Bcw_],
                          allow_slow_non_contiguous=True)
                    P.dma("sp", cw_[:, 8:12], conv_v[:, h * 256:h * 256 + 128].rearrange("j c -> c j"), writes=[Bcw_],
                          allow_slow_non_contiguous=True)
                    P.dma("sp", cw_[:, 12:16], conv_v[:, h * 256 + 128:h * 256 + 256].rearrange("j c -> c j"), writes=[Bcw_],
                          allow_slow_non_contiguous=True)
                    for i in range(16):
                        ts("dve", dg_[:, i, :], identf[:], cw_[:, i:i + 1], ALU.mult, [Bcw_, Bc], [Bdg_])

                Ws = {0: load_head_w(0), 1: load_head_w(1)}
                seq = [(h, c) for h in range(8) for c in range(NCH)]
                NSEQ = len(seq)
                ready.clear()
                active = []
                free_slots = list(range(G))
                next_i, next_scan, next_started = 0, 0, 0
                scans = []
                live = {}
                started_heads = set()
                st_["state_done"] = True
                while next_started < NSEQ or not st_["state_done"]:
                    while free_slots and next_i < NSEQ and next_i < next_scan + NRC:
                        h, c = seq[next_i]
                        if h not in Ws:
                            break
                        if h not in started_heads:
                            started_heads.add(h)
                            head_setup(h)
                        si = free_slots.pop(0)
                        W, bW = Ws[h]
                        active.append([next_i, prep(h, c, si, W, bW, next_i, dgs[h % 2]), si, h])
                        live[h] = live.get(h, 0) + 1
                        next_i += 1
                    for item in list(active):
                        try:
                            next(item[1])
                        except StopIteration:
                            active.remove(item)
                            free_slots.append(item[2])
                            hh = item[3]
                            live[hh] -= 1
                            if (live[hh] == 0 and (next_i >= NSEQ or seq[next_i][0] > hh) and hh + 2 < 8
                                    and (hh + 2) not in Ws):
                                Ws[hh + 2] = load_head_w(hh + 2)
                    for sg in list(scans):
                        try:
                            next(sg)
                        except StopIteration:
                            scans.remove(sg)
                            next_scan += 1
                    if next_started in ready and st_["state_done"]:
                        st_["state_done"] = False
                        h, c = seq[next_started]
                        scans.append(scan(h, c, ready.pop(next_started)))
                        next_started += 1
                for sg in scans:
                    for _ in sg:
                        pass
                P.barrier()

        if do_sb:
            with contextlib.ExitStack() as ph:
                sb, ps = alloc_in(ph)
                WsR = Rot(sb, "Ws", [128, KT, 384], BF16, 2)
                gq = sb("gq", [128, 1], F32)
                gk = sb("gk", [128, 1], F32)
                Bgq = Buf()
                P.dma("sp", gq[:], g_sbq.rearrange("(d o) -> d o", o=1), writes=[Bgq])
                P.dma("sp", gk[:], g_sbk.rearrange("(d o) -> d o", o=1), writes=[Bgq])
                ts("dve", gq[:], gq[:], 128.0 ** -0.5, ALU.mult, [Bgq], [Bgq])
                HB = []
                for i in range(2):
                    HB.append(dict(qT=sb("sqT%d" % i, [128, NPOS], BF16), kT=sb("skT%d" % i, [128, NPOS], BF16),
                                   nkT=sb("snkT%d" % i, [128, NPOS], BF16), vS=sb("svS%d" % i, [128, NCH, 128], BF16),
                                   Bq=Buf(), Bk=Buf(), Bv=Buf()))
                pq, Bpq = ps("pq", [128, 512], F32), Buf("pq", True)
                pss, Bpss = ps("pss", [128, 512], F32), Buf("pss", True)
                NL = 3
                lanes = []
                for li in range(NL):
                    lanes.append(dict(
                        ZC=(ps("pZC%d" % li, [128, 512], F32), Buf("pZC%d" % li, True)),
                        OT=(ps("pOT%d" % li, [128, 512], F32), Buf("pOT%d" % li, True)),
                        R=(sb("Rr%d" % li, [128, 512], BF16), Buf()),
                        e=Rot(sb, "ce%d" % li, [128, 512], F32, 2),
                        sp=Rot(sb, "csp%d" % li, [128, 512], BF16, 2),
                        w=Rot(sb, "cw%d_" % li, [128, 512], BF16, 2),
                        ot=Rot(sb, "cot%d" % li, [128, 512], BF16, 1)))
                r_raw = Rot(sb, "craw", [128, 512], F32, 2)
                r_sq = Rot(sb, "csq", [128, 512], BF16, 2)
                r_ln = Rot(sb, "cln", [128, 512], F32, 2)

                def load_sb_w(h):
                    W, bW = WsR.next()
                    for (o0, src0) in ((0, OFF_SQ + h * 128), (128, OFF_SK + h * 128), (256, OFF_SV + h * 128)):
                        P.dma("pool", W[:, :, o0:o0 + 128], w_in[:, src0:src0 + 128].rearrange("(k p) n -> p k n", p=128),
                              writes=[bW])
                    return W, bW

                def proj(h, W, bW):
                    H = HB[h % 2]
                    kT, nkT, vS = H["kT"], H["nkT"], H["vS"]
                    Bv = H["Bv"]
                    for (which, o0, dstT, gcol, bd) in (("q", 0, H["qT"], gq, H["Bq"]), ("k", 128, H["kT"], gk, H["Bk"])):
                        for p0 in range(0, NPOS, 512):
                            n = min(512, NPOS - p0)
                            for k in range(KT):
                                mm(pq[:, 0:n], W[:, k, o0:o0 + 128], hnT[:, k, p0:p0 + n], k == 0, k == KT - 1, [bW], [Bpq])
                            yield
                            rw, brw = r_raw.next()
                            sq, bsq = r_sq.next()
                            act(rw[:, 0:n], pq[:, 0:n], AF.Copy, [Bpq], [brw])
                            act(sq[:, 0:n], pq[:, 0:n], AF.Square, [Bpq], [bsq])
                            yield
                            mm(pss[:, 0:n], ONESb[:], sq[:, 0:n], True, True, [bsq, Bc], [Bpss])
                            yield
                            ln, bln = r_ln.next()
                            act(ln[:, 0:n], pss[:, 0:n], AF.Ln, [Bpss], [bln], scale=1.0 / 128, bias=RMS_EPS)
                            yield
                            act(ln[:, 0:n], ln[:, 0:n], AF.Exp, [bln], [bln], scale=-0.5)
                            yield
                            stt(dstT[:, p0:p0 + n], rw[:, 0:n], gcol[:, 0:1], ln[:, 0:n], ALU.mult, ALU.mult,
                                [brw, bln, Bgq], [bd])
                            if which == "k":
                                yield
                                ts("dve", nkT[:, p0:p0 + n], kT[:, p0:p0 + n], -1.0, ALU.mult, [bd], [bd])
                            yield
                    for c in range(NCH):
                        C = NM if c == 0 else 128
                        p0 = 0 if c == 0 else NM + 128 * (c - 1)
                        for k in range(KT):
                            mm(pq[0:C, 0:128], hnT[:, k, p0:p0 + C], W[:, k, 256:384], k == 0, k == KT - 1, [bW], [Bpq])
                        yield
                        cp("dve", vS[0:C, c, :], pq[0:C, 0:128], [Bpq], [Bv])
                        yield

                W0 = load_sb_w(0)
                for _ in proj(0, *W0):
                    pass
                for h in range(8):
                    H = HB[h % 2]
                    qT, kT, nkT, vS = H["qT"], H["kT"], H["nkT"], H["vS"]
                    Bq, Bk, Bv = H["Bq"], H["Bk"], H["Bv"]
                    pgen = None
                    if h + 1 < 8:
                        Wn = load_sb_w(h + 1)
                        pgen = proj(h + 1, *Wn)
                    def attn(g, L):
                        qp0 = NM + 512 * g
                        blocks = [(b_, b_ - 4 * g) for b_ in range(4 * g + 3, -1, -1)] + [(-1, -1)]
                        ZC, bZC = L["ZC"]
                        OT, bOT = L["OT"]
                        Rr, BR = L["R"]
                        nb = len(blocks)
                        mm(OT[:, :], Zb[:, :], qT[:, qp0:qp0 + 512], True, False, [Bq, Bc], [bOT], counted=False)
                        for i, (b_, r) in enumerate(blocks):
                            c0 = 128 * r if r > 0 else 0
                            n = 512 - c0
                            C = NM if b_ < 0 else 128
                            kp0 = 0 if b_ < 0 else NM + 128 * b_
                            mm(ZC[0:C, 0:n], kT[:, kp0:kp0 + C], qT[:, qp0 + c0:qp0 + 512], True, True, [Bk, Bq], [bZC])
                            yield
                            e, be = L["e"].next()
                            sp, bsp = L["sp"].next()
                            act(e[0:C, 0:n], ZC[0:C, 0:n], AF.Exp, [bZC], [be])
                            yield
                            act(sp[0:C, 0:n], e[0:C, 0:n], AF.Ln, [be], [bsp], bias=1.0)
                            yield
                            if r >= 0:
                                asel(sp[0:C, 0:128], [[1, 128]], ALU.is_gt, 0.0, -1, [bsp])
                                yield
                            mm(ZC[0:C, 0:n], TRIb[0:C, 0:C], sp[0:C, 0:n], True, False, [bsp, Bc], [bZC], counted=False)
                            cs = 128 if r >= 0 else 0
                            if n - cs > 0 and i > 0:
                                mm(ZC[0:C, cs:n], ONESb[:, 0:C], Rr[:, c0 + cs:512], False, False, [BR, Bc], [bZC], counted=False)
                            mm(ZC[0:C, 0:n], nkT[:, kp0:kp0 + C], qT[:, qp0 + c0:qp0 + 512], False, True, [Bk, Bq], [bZC])
                            yield
                            if i < nb - 1:
                                if r >= 0:
                                    cp("dve", Rr[:, c0:c0 + 128], sp[:, 0:128], [bsp], [BR])
                                    if n > 128:
                                        tt("dve", Rr[:, c0 + 128:512], Rr[:, c0 + 128:512], sp[:, 128:n], ALU.add, [bsp, BR], [BR])
                                else:
                                    tt("dve", Rr[:, :], Rr[:, :], sp[:, 0:512], ALU.add, [bsp, BR], [BR])
                            w, bw = L["w"].next()
                            act(w[0:C, 0:n], ZC[0:C, 0:n], AF.Exp, [bZC], [bw], scale=-1.0)
                            yield
                            if r >= 0:
                                asel(w[0:C, 0:128], [[1, 128]], ALU.is_gt, 0.0, -1, [bw])
                                yield
                            last = (i == nb - 1)
                            mm(OT[:, c0:512], vS[0:C, b_ + 1, :], w[0:C, 0:n], False, last, [Bv, bw], [bOT], counted=last)
                            yield
                        ot, bot = L["ot"].next()
                        cp("dve", ot[:], OT[:], [bOT], [bot])
                        yield
                        P.dma("sp", osbT[h * 128:(h + 1) * 128, 512 * g:512 * (g + 1)], ot[:], reads=[bot])

                    todo = list(range(NG - 1, -1, -1))
                    active = []
                    free_l = list(range(NL))
                    while todo or active or pgen is not None:
                        while free_l and todo:
                            li = free_l.pop(0)
                            active.append([attn(todo.pop(0), lanes[li]), li])
                        for item in list(active):
                            try:
                                next(item[0])
                            except StopIteration:
                                active.remove(item)
                                free_l.append(item[1])
                        if pgen is not None:
                            try:
                                next(pgen)
                            except StopIteration:
                                pgen = None
                P.barrier()
        else:
            P.barrier()
        mid.close()

        with contextlib.ExitStack() as ph:
            sb, ps = alloc_in(ph)
            Wg = sb("Wg", [128, KT, 2048], BF16)
            Wbd = sb("Wbd", [128, 16, D], BF16)
            Wbs = sb("Wbs", [128, 8, D], BF16)
            Wo = sb("Wo", [128, 8, D], BF16)
            BWg, BWbd, BWbs, BWo = Buf(), Buf(), Buf(), Buf()
            for c2 in range(2):
                P.dma("pool", Wg[:, :, c2 * 1024:(c2 + 1) * 1024],
                      w_in[:, OFF_GDN + c2 * 1024:OFF_GDN + (c2 + 1) * 1024].rearrange("(k p) n -> p k n", p=128), writes=[BWg])
            for c2 in range(2):
                P.dma("pool", Wbd[:, c2 * 8:(c2 + 1) * 8, :],
                      w_bdn[c2 * 1024:(c2 + 1) * 1024, :].rearrange("(k p) n -> p k n", p=128), writes=[BWbd])
            P.dma("pool", Wbs[:], w_bsb.rearrange("(k p) n -> p k n", p=128), writes=[BWbs])
            P.dma("pool", Wo[:], w_out.rearrange("(k p) n -> p k n", p=128), writes=[BWo])
            g1 = sb("g1d", [128, KT], F32)
            Bg1 = Buf()
            P.dma("sp", g1[:], g_mix.rearrange("(k p) -> p k", p=128), writes=[Bg1], allow_slow_non_contiguous=True)
            R = {"junk": Rot(sb, "junkD", [128, D], BF16, 1), "ss": Rot(sb, "ssD", [128, 1], F32, 2),
                 "xs": Rot(sb, "xsD", [128, D], BF16, 2), "pT": Rot(ps, "pTD", [128, D], BF16, 1, excl=True)}
            x4R = Rot(sb, "x4", [128, 4, D], F32, 1)
            hsR = Rot(sb, "hs", [128, KT, 512], BF16, 2)
            odR = Rot(sb, "od", [128, 16, 512], BF16, 1)
            osR = Rot(sb, "os", [128, 8, 512], BF16, 1)
            mTR = Rot(sb, "mT", [128, 8, 512], BF16, 1)
            thR = Rot(sb, "thD", [128, 512], F32, 2)
            t1R = Rot(sb, "t1D", [128, 512], F32, 2)
            xrR = Rot(sb, "xres", [128, D], F32, 3)
            pG = Rot(ps, "pG", [128, 512], F32, 7, excl=True)
            pO = pG
            PROD = {}
            OLD = {}

            def proD(s_):
                t0 = 512 * s_
                x4, bx4 = x4R.next()
                for ts_ in range(4):
                    P.dma("sp", x4[:, ts_, :], x[t0 + 128 * ts_:t0 + 128 * (ts_ + 1), :], writes=[bx4])
                hs, bhs = hsR.next()
                for ts_ in range(4):
                    def dst(k, ts_=ts_, hs=hs, bhs=bhs):
                        return hs[:, k, ts_ * 128:(ts_ + 1) * 128], [bhs]
                    yield from rmsnorm_T_gen(sb, ps, "D", 128, x4[:, ts_, :], g1, dst, R, bx4, Bg1)
                    yield
                PROD[s_] = (hs, bhs)

            def load_o(s_):
                t0 = 512 * s_
                od, bod = odR.next()
                osb_, bos = osR.next()
                if do_dn:
                    P.dma("sp", od[:], odnT[:, t0:t0 + 512].rearrange("(k p) t -> p k t", p=128), writes=[bod])
                else:
                    memset("dve", od[:], 0.0, [bod])
                if do_sb:
                    P.dma("sp", osb_[:], osbT[:, t0:t0 + 512].rearrange("(k p) t -> p k t", p=128), writes=[bos])
                else:
                    memset("dve", osb_[:], 0.0, [bos])
                OLD[s_] = (od, bod, osb_, bos)

            def load_xr(s_, ts_):
                t0 = 512 * s_
                xr, bxr = xrR.next()
                P.dma("sp", xr[:], x[t0 + 128 * ts_:t0 + 128 * (ts_ + 1), :], writes=[bxr])
                return xr, bxr

            def mainD(s_):
                t0 = 512 * s_
                hs, bhs = PROD.pop(s_)
                od, bod, osb_, bos = OLD.pop(s_)
                mT, bmT = mTR.next()
                for dt_ in range(8):
                    pgd_, bpgd = pG.next()
                    pgs_, bpgs = pG.next()
                    pA, bpA = pG.next()
                    pB, bpB = pG.next()
                    for k in range(KT):
                        mm(pgd_[:], Wg[:, k, dt_ * 128:(dt_ + 1) * 128], hs[:, k, :], k == 0, k == KT - 1, [BWg, bhs], [bpgd])
                    for k in range(KT):
                        mm(pgs_[:], Wg[:, k, 1024 + dt_ * 128:1024 + (dt_ + 1) * 128], hs[:, k, :], k == 0, k == KT - 1,
                           [BWg, bhs], [bpgs])
                    for k in range(16):
                        mm(pA[:], Wbd[:, k, dt_ * 128:(dt_ + 1) * 128], od[:, k, :], k == 0, k == 15, [BWbd, bod], [bpA])
                    for k in range(8):
                        mm(pB[:], Wbs[:, k, dt_ * 128:(dt_ + 1) * 128], osb_[:, k, :], k == 0, k == 7, [BWbs, bos], [bpB])
                    thd, bthd = thR.next()
                    ths, bths = thR.next()
                    act(thd[:], pgd_[:], AF.Tanh, [bpgd], [bthd], scale=0.5)
                    act(ths[:], pgs_[:], AF.Tanh, [bpgs], [bths], scale=0.5)
                    t1, bt1 = t1R.next()
                    t2, bt2 = t1R.next()
                    stt(t1[:], thd[:], 1.0, pA[:], ALU.add, ALU.mult, [bthd, bpA], [bt1])
                    stt(t2[:], ths[:], 1.0, pB[:], ALU.add, ALU.mult, [bths, bpB], [bt2])
                    tt("dve", mT[:, dt_, :], t1[:], t2[:], ALU.add, [bt1, bt2], [bmT])
                    yield
                if s_ + 1 < NG:
                    load_o(s_ + 1)
                xq = [load_xr(s_, 0), load_xr(s_, 1)]
                for ts_ in range(4):
                    xr, bxr = xq.pop(0)
                    for ch in range(2):
                        po, bpo = pO.next()
                        for k in range(8):
                            mm(po[:], mT[:, k, ts_ * 128:(ts_ + 1) * 128], Wo[:, k, ch * 512:(ch + 1) * 512], k == 0, k == 7,
                               [bmT, BWo], [bpo])
                        stt(xr[:, ch * 512:(ch + 1) * 512], po[:], 0.5, xr[:, ch * 512:(ch + 1) * 512], ALU.mult, ALU.add,
                            [bpo, bxr], [bxr])
                        yield
                    P.dma("sp", h1[t0 + 128 * ts_:t0 + 128 * (ts_ + 1), :], xr[:], reads=[bxr])
                    if ts_ + 2 < 4:
                        xq.append(load_xr(s_, ts_ + 2))

            load_o(0)
            for _ in proD(0):
                pass
            for s_ in range(NG):
                gens = [mainD(s_)]
                if s_ + 1 < NG:
                    gens.append(proD(s_ + 1))
                while gens:
                    for g_ in list(gens):
                        try:
                            next(g_)
                        except StopIteration:
                            gens.remove(g_)
            P.barrier()

        with contextlib.ExitStack() as ph:
            sb, ps = alloc_in(ph)
            SW = 256
            Wfi = sb("Wfi", [128, KT, 2 * DFF], BF16)
            Wfo = sb("Wfo", [128, NFF, D], BF16)
            BWf = [Buf() for _ in range(4)]
            BWo2 = [Buf(), Buf()]
            for c2 in (0, 2, 1, 3):
                P.dma("pool", Wfi[:, :, c2 * 1408:(c2 + 1) * 1408],
                      w_fi[:, c2 * 1408:(c2 + 1) * 1408].rearrange("(k p) n -> p k n", p=128), writes=[BWf[c2]])
            P.dma("pool", Wfo[:, 0:11, :], w_fo[0:1408, :].rearrange("(k p) n -> p k n", p=128), writes=[BWo2[0]])
            P.dma("pool", Wfo[:, 11:22, :], w_fo[1408:2816, :].rearrange("(k p) n -> p k n", p=128), writes=[BWo2[1]])
            g2 = sb("g2", [128, KT], F32)
            Bg2 = Buf()
            P.dma("sp", g2[:], g_ffn.rearrange("(k p) -> p k", p=128), writes=[Bg2], allow_slow_non_contiguous=True)
            R = {"junk": Rot(sb, "junkE", [128, D], BF16, 1), "ss": Rot(sb, "ssE", [128, 1], F32, 2),
                 "xs": Rot(sb, "xsE", [128, D], BF16, 2), "pT": Rot(ps, "pTE", [128, D], BF16, 1, excl=True)}
            NS = SW // 128
            h4R = Rot(sb, "h4", [128, NS, D], F32, 2)
            hsR = Rot(sb, "hs2", [128, KT, SW], BF16, 2)
            aTR = Rot(sb, "aT", [128, NFF, SW], BF16, 1)
            thR = Rot(sb, "thE", [128, SW], F32, 3)
            t1R = Rot(sb, "t1E", [128, SW], F32, 3)
            hoR = Rot(sb, "ho2", [128, D], F32, 2)
            pG = Rot(ps, "pG2", [128, 512], F32, 7, excl=True)
            pO = pG
            PRO = {}

            def pro(s_):
                t0 = SW * s_
                h4, bh4 = h4R.next()
                for ts_ in range(NS):
                    P.dma("sp", h4[:, ts_, :], h1[t0 + 128 * ts_:t0 + 128 * (ts_ + 1), :], writes=[bh4])
                hs, bhs = hsR.next()
                for ts_ in range(NS):
                    def dst(k, ts_=ts_, hs=hs, bhs=bhs):
                        return hs[:, k, ts_ * 128:(ts_ + 1) * 128], [bhs]
                    yield from rmsnorm_T_gen(sb, ps, "E", 128, h4[:, ts_, :], g2, dst, R, bh4, Bg2)
                    yield
                PRO[s_] = (h4, bh4, hs, bhs)

            def main(s_):
                t0 = SW * s_
                h4, bh4, hs, bhs = PRO.pop(s_)
                aT, baT = aTR.next()
                for f in range(NFF):
                    pg, bpg = pG.next()
                    pu, bpu = pG.next()
                    for k in range(KT):
                        mm(pg[:, 0:SW], Wfi[:, k, f * 128:(f + 1) * 128], hs[:, k, :], k == 0, k == KT - 1, [BWf[f // 11], bhs], [bpg])
                    for k in range(KT):
                        mm(pu[:, 0:SW], Wfi[:, k, DFF + f * 128:DFF + (f + 1) * 128], hs[:, k, :], k == 0, k == KT - 1,
                           [BWf[2 + f // 11], bhs], [bpu])
                    th, bth = thR.next()
                    act(th[:], pg[:, 0:SW], AF.Tanh, [bpg], [bth], scale=0.5)
                    t1, bt1 = t1R.next()
                    stt(t1[:], th[:], 1.0, pg[:, 0:SW], ALU.add, ALU.mult, [bth, bpg], [bt1])
                    tt("dve", aT[:, f, :], t1[:], pu[:, 0:SW], ALU.mult, [bt1, bpu], [baT])
                    yield
                for ts_ in range(NS):
                    ho, bho = hoR.next()
                    for ch in range(2):
                        po, bpo = pO.next()
                        for f in range(NFF):
                            mm(po[:], aT[:, f, ts_ * 128:(ts_ + 1) * 128], Wfo[:, f, ch * 512:(ch + 1) * 512], f == 0, f == NFF - 1,
                               [baT, BWo2[f // 11]], [bpo])
                        stt(ho[:, ch * 512:(ch + 1) * 512], po[:], 0.5, h4[:, ts_, ch * 512:(ch + 1) * 512], ALU.mult, ALU.add,
                            [bpo, bh4], [bho])
                        yield
                    P.dma("sp", out[t0 + 128 * ts_:t0 + 128 * (ts_ + 1), :], ho[:], reads=[bho])

            NSUP = NT // SW
            for _ in pro(0):
                pass
            for s_ in range(NSUP):
                gens = [main(s_)]
                if s_ + 1 < NSUP:
                    gens.append(pro(s_ + 1))
                while gens:
                    for g_ in list(gens):
                        try:
                            next(g_)
                        except StopIteration:
                            gens.remove(g_)
            P.barrier()
        P.emit()
    return nc


_CACHE = {}


def _names():
    return dict(x="x", meta_tokens="meta", norm_mix_gain="g_mix", w_in="w_in", conv_q="conv_q", conv_k="conv_k",
                conv_v="conv_v", dn_a_log="a_log", dn_dt_bias="dt_bias", dn_out_norm_gain="g_dn",
                sb_q_norm_gain="g_sbq", sb_k_norm_gain="g_sbk", w_branch_dn="w_bdn", w_branch_sb="w_bsb",
                w_out="w_out", norm_ffn_gain="g_ffn", w_ffn_in="w_fi", w_ffn_out="w_fo")


def run(inputs, **bkw):
    xs = np.ascontiguousarray(np.asarray(inputs["x"], dtype=np.float32))
    B, NT, _ = xs.shape
    shared = {}
    for k, v in inputs.items():
        if k == "x":
            continue
        a = np.asarray(v, dtype=np.float32)
        if k != "meta_tokens":
            a = a[0]
        shared[_names()[k]] = np.ascontiguousarray(a)
    nc = build(NT, **bkw)
    in_maps = [dict(shared, x=xs[b]) for b in range(B)]
    res = run_bass_kernel_spmd(nc, in_maps, core_ids=list(range(B)))
    return res


def kernel(**inputs):
    res = run(inputs)
    return np.stack([np.asarray(r["out"], dtype=np.float32) for r in res.results], axis=0)
```
